# Optimizing a Trainium2 kernel written in Bass

```python
import jax, jax.numpy as jnp
from jax import lax
import numpy as np

D_MODEL = 2048
BATCH = 2
SEQ = 8192
DEPTH = 4

MIX_WIDTH = D_MODEL
BLOCK = 128
WINDOW = 128
SWA_HEAD_DIM = 64
SWA_Q_HEADS = (MIX_WIDTH // 2) // SWA_HEAD_DIM
SWA_GROUP = 4
SWA_KV_HEADS = SWA_Q_HEADS // SWA_GROUP
MLA_NOPE_DIM = 128
MLA_ROPE_DIM = 64
MLA_QK_DIM = MLA_NOPE_DIM + MLA_ROPE_DIM
MLA_V_DIM = 128
MLA_HEADS = (MIX_WIDTH // 2) // MLA_V_DIM
MLA_Q_RANK = D_MODEL // 4
MLA_KV_RANK = D_MODEL // 8
ROPE_THETA = 10000.0
D_FF = 5632
PLE_DIM = 256
EPS = 1e-6

SWA_Q_COLS = SWA_Q_HEADS * SWA_HEAD_DIM
SWA_KV_COLS = SWA_KV_HEADS * SWA_HEAD_DIM
_SIZES = (SWA_Q_COLS, SWA_KV_COLS, SWA_KV_COLS, MLA_Q_RANK, MLA_KV_RANK, MLA_ROPE_DIM)
IN_COLS = sum(_SIZES)
IN_SPLITS = tuple(int(v) for v in np.cumsum(_SIZES)[:-1])

kernel_name = "hybrid_swa_sink_mla_macaron_ple"


def rms_norm(x, g):
    xf = x.astype(jnp.float32)
    y = xf * lax.rsqrt(jnp.mean(xf * xf, axis=-1, keepdims=True) + EPS)
    return (y * g.astype(jnp.float32)).astype(x.dtype)


def swiglu(x, w_gate, w_up, w_down):
    return (jax.nn.silu(x @ w_gate) * (x @ w_up)) @ w_down


def alibi_slopes(n_heads):
    h = np.arange(1, n_heads + 1, dtype=np.float32)
    return jnp.asarray(2.0 ** (-8.0 * h / n_heads), dtype=jnp.float32)


def apply_rope(x, positions):
    half = x.shape[-1] // 2
    inv_freq = ROPE_THETA ** (-jnp.arange(half, dtype=jnp.float32) / half)
    ang = positions.astype(jnp.float32)[:, :, None] * inv_freq
    cos = jnp.cos(ang)[:, :, None, :]
    sin = jnp.sin(ang)[:, :, None, :]
    xf = x.astype(jnp.float32)
    x1, x2 = xf[..., :half], xf[..., half:]
    return jnp.concatenate([x1 * cos - x2 * sin, x2 * cos + x1 * sin], axis=-1).astype(x.dtype)


def _with_prev_block(t, nb):
    tb = t.reshape(t.shape[0], nb, BLOCK, *t.shape[2:])
    prev = jnp.pad(tb[:, :-1], [(0, 0), (1, 0)] + [(0, 0)] * (tb.ndim - 2))
    return jnp.concatenate([prev, tb], axis=2)


def sliding_window_attention(q, k, v, positions, sinks):
    B, S = q.shape[0], q.shape[1]
    nb = S // BLOCK
    qb = q.reshape(B, nb, BLOCK, SWA_KV_HEADS, SWA_GROUP, SWA_HEAD_DIM)
    kb = _with_prev_block(k, nb)
    vb = _with_prev_block(v, nb)
    pk = _with_prev_block(positions, nb)
    pq = positions.reshape(B, nb, BLOCK)
    s = jnp.einsum('bnqhgd,bnkhd->bnhgqk', qb, kb,
                   preferred_element_type=jnp.float32) * (SWA_HEAD_DIM ** -0.5)
    dist = jnp.abs(pq[..., :, None] - pk[..., None, :]).astype(jnp.float32)
    slopes = alibi_slopes(SWA_Q_HEADS).reshape(SWA_KV_HEADS, SWA_GROUP)
    s = s - slopes[None, None, :, :, None, None] * dist[:, :, None, None]
    qi = jnp.arange(BLOCK)[:, None]
    kj = jnp.arange(2 * BLOCK)[None, :]
    diff = qi + BLOCK - kj
    band = (diff >= 0) & (diff < WINDOW)
    exists = (jnp.arange(nb)[:, None, None] > 0) | (kj[None] >= BLOCK)
    valid = band[None] & exists
    s = jnp.where(valid[None, :, None, None], s, -jnp.inf)
    sink = sinks.astype(jnp.float32).reshape(SWA_KV_HEADS, SWA_GROUP)[None, None, :, :, None, None]
    m = jnp.maximum(jnp.max(s, axis=-1, keepdims=True), sink)
    e = jnp.exp(s - m)
    probs = e / (jnp.sum(e, axis=-1, keepdims=True) + jnp.exp(sink - m))
    o = jnp.einsum('bnhgqk,bnkhd->bnqhgd', probs.astype(v.dtype), vb)
    return o.reshape(B, S, SWA_Q_HEADS * SWA_HEAD_DIM)


def causal_block_attention(q, k, v):
    B, S = q.shape[0], q.shape[1]
    nb = S // BLOCK
    qb = jnp.moveaxis(q.reshape(B, nb, BLOCK, MLA_HEADS, MLA_QK_DIM), 1, 0)
    kidx = jnp.arange(S)
    scale = MLA_QK_DIM ** -0.5

    def one_block(args):
        qblk, n = args
        s = jnp.einsum('bqhd,bkhd->bhqk', qblk, k, preferred_element_type=jnp.float32) * scale
        qidx = n * BLOCK + jnp.arange(BLOCK)
        s = jnp.where(kidx[None, :] <= qidx[:, None], s, -jnp.inf)
        probs = jax.nn.softmax(s, axis=-1)
        return jnp.einsum('bhqk,bkhd->bqhd', probs.astype(v.dtype), v)

    o = lax.map(one_block, (qb, jnp.arange(nb)))
    return jnp.moveaxis(o, 0, 1).reshape(B, S, MLA_HEADS * MLA_V_DIM)


def setup_inputs(seed: int = 0) -> dict:
    key = jax.random.key(seed)
    ks = iter(jax.random.split(key, 40))
    f32 = jnp.float32

    def w(shape, fan_in):
        return jax.random.normal(next(ks), shape, f32) * (fan_in ** -0.5)

    def gain(dim):
        return 1.0 + 0.05 * jax.random.normal(next(ks), (DEPTH, dim), f32)

    x = jax.random.normal(next(ks), (BATCH, SEQ, D_MODEL), f32)
    p = jax.random.normal(next(ks), (DEPTH, BATCH, SEQ, PLE_DIM), f32)
    offset = jax.random.randint(next(ks), (BATCH, 1), 0, 1024, dtype=jnp.int32)
    positions = (offset + jnp.arange(SEQ, dtype=jnp.int32)[None, :]).astype(jnp.int32)
    return {
        "x": x,
        "p": p,
        "positions": positions,
        "ffn1_norm": gain(D_MODEL),
        "ffn1_w_gate": w((DEPTH, D_MODEL, D_FF), D_MODEL),
        "ffn1_w_up": w((DEPTH, D_MODEL, D_FF), D_MODEL),
        "ffn1_w_down": w((DEPTH, D_FF, D_MODEL), D_FF),
        "mix_norm": gain(D_MODEL),
        "w_in": w((DEPTH, D_MODEL, IN_COLS), D_MODEL),
        "swa_q_norm": gain(SWA_HEAD_DIM),
        "swa_k_norm": gain(SWA_HEAD_DIM),
        "swa_sinks": 0.5 * jax.random.normal(next(ks), (DEPTH, SWA_Q_HEADS), f32),
        "mla_q_lora_norm": gain(MLA_Q_RANK),
        "mla_w_uq": w((DEPTH, MLA_Q_RANK, MLA_HEADS * MLA_QK_DIM), MLA_Q_RANK),
        "mla_kv_lora_norm": gain(MLA_KV_RANK),
        "mla_w_ukv": w((DEPTH, MLA_KV_RANK, MLA_HEADS * (MLA_NOPE_DIM + MLA_V_DIM)), MLA_KV_RANK),
        "mla_q_norm": gain(MLA_QK_DIM),
        "mla_k_norm": gain(MLA_QK_DIM),
        "out_norm_swa": gain(SWA_Q_COLS),
        "out_norm_mla": gain(MLA_HEADS * MLA_V_DIM),
        "w_o": w((DEPTH, MIX_WIDTH, D_MODEL), MIX_WIDTH),
        "ffn2_norm": gain(D_MODEL),
        "ffn2_w_gate": w((DEPTH, D_MODEL, D_FF), D_MODEL),
        "ffn2_w_up": w((DEPTH, D_MODEL, D_FF), D_MODEL),
        "ffn2_w_down": w((DEPTH, D_FF, D_MODEL), D_FF),
        "ple_proj": w((DEPTH, PLE_DIM, D_MODEL), PLE_DIM),
        "ple_proj_norm": gain(D_MODEL),
        "ple_gate_norm": gain(D_MODEL),
        "ple_gate": w((DEPTH, D_MODEL, D_MODEL), D_MODEL),
    }


def reference(x, p, positions, ffn1_norm, ffn1_w_gate, ffn1_w_up, ffn1_w_down,
              mix_norm, w_in, swa_q_norm, swa_k_norm, swa_sinks,
              mla_q_lora_norm, mla_w_uq, mla_kv_lora_norm, mla_w_ukv, mla_q_norm, mla_k_norm,
              out_norm_swa, out_norm_mla, w_o,
              ffn2_norm, ffn2_w_gate, ffn2_w_up, ffn2_w_down,
              ple_proj, ple_proj_norm, ple_gate_norm, ple_gate):
    B, S = x.shape[0], x.shape[1]
    for i in range(DEPTH):
        h = x + 0.5 * swiglu(rms_norm(x, ffn1_norm[i]), ffn1_w_gate[i], ffn1_w_up[i], ffn1_w_down[i])

        n = rms_norm(h, mix_norm[i])
        z = n @ w_in[i]
        q_a, k_a, v_a, c_q, c_kv, k_r = jnp.split(z, IN_SPLITS, axis=-1)

        q_a = rms_norm(q_a.reshape(B, S, SWA_Q_HEADS, SWA_HEAD_DIM), swa_q_norm[i])
        k_a = rms_norm(k_a.reshape(B, S, SWA_KV_HEADS, SWA_HEAD_DIM), swa_k_norm[i])
        v_a = v_a.reshape(B, S, SWA_KV_HEADS, SWA_HEAD_DIM)
        o_a = sliding_window_attention(q_a, k_a, v_a, positions, swa_sinks[i])

        q_b = (rms_norm(c_q, mla_q_lora_norm[i]) @ mla_w_uq[i]).reshape(B, S, MLA_HEADS, MLA_QK_DIM)
        kv = (rms_norm(c_kv, mla_kv_lora_norm[i]) @ mla_w_ukv[i]).reshape(
            B, S, MLA_HEADS, MLA_NOPE_DIM + MLA_V_DIM)
        k_nope, v_b = kv[..., :MLA_NOPE_DIM], kv[..., MLA_NOPE_DIM:]
        k_rope = jnp.broadcast_to(k_r[:, :, None, :], (B, S, MLA_HEADS, MLA_ROPE_DIM))
        k_b = jnp.concatenate([k_nope, k_rope], axis=-1)
        q_b = rms_norm(q_b, mla_q_norm[i])
        k_b = rms_norm(k_b, mla_k_norm[i])
        q_b = jnp.concatenate([q_b[..., :MLA_NOPE_DIM], apply_rope(q_b[..., MLA_NOPE_DIM:], positions)], axis=-1)
        k_b = jnp.concatenate([k_b[..., :MLA_NOPE_DIM], apply_rope(k_b[..., MLA_NOPE_DIM:], positions)], axis=-1)
        o_b = causal_block_attention(q_b, k_b, v_b)

        mixed = jnp.concatenate([rms_norm(o_a, out_norm_swa[i]), rms_norm(o_b, out_norm_mla[i])], axis=-1)
        h = h + mixed @ w_o[i]

        h = h + 0.5 * swiglu(rms_norm(h, ffn2_norm[i]), ffn2_w_gate[i], ffn2_w_up[i], ffn2_w_down[i])

        gate = jax.nn.sigmoid(rms_norm(h, ple_gate_norm[i]) @ ple_gate[i])
        x = h + rms_norm(p[i] @ ple_proj[i], ple_proj_norm[i]) * gate
    return x
```

```python
import math
from contextlib import ExitStack

import numpy as np
import concourse.bass as bass
import concourse.mybir as mybir
from concourse.bass_utils import run_bass_kernel_spmd

F32 = mybir.dt.float32
BF16 = mybir.dt.bfloat16
I32 = mybir.dt.int32
AF = mybir.ActivationFunctionType
ALU = mybir.AluOpType

ENGS = ("pe", "act", "dve", "pool", "sp")

D = 2048; DFF = 5632; INC = 2368; PLE = 256
DC = D // 128; FC = DFF // 128
EPS = 1e-6
DEPTH = 4; BATCH = 2; SEQ = 8192
TT = 512
PI = math.pi


class Buf:
    __slots__ = ("name", "w", "r", "chan")

    def __init__(self, name):
        self.name = name
        self.w = {}
        self.r = {}
        self.chan = None


class Sched:
    def __init__(self, nc):
        self.nc = nc
        self.ops = {e: [] for e in ENGS}
        self.nchan = 0
        self.chan_cnt = []
        self.miles = {e: set() for e in ENGS}
        self.pending_barrier = {e: None for e in ENGS}
        self.chan_by_name = {}

    @staticmethod
    def _merge(d, key, val):
        if d.get(key, -1) < val:
            d[key] = val

    def _collect(self, eng, reads, writes):
        deps = {}
        for b in reads:
            for k, v in b.w.items():
                self._merge(deps, k, v)
        for b in writes:
            for k, v in b.w.items():
                self._merge(deps, k, v)
            for k, v in b.r.items():
                self._merge(deps, k, v)
        own = ("e", eng)
        if own in deps:
            raw = -1
            if eng != "pe":
                for b in reads:
                    raw = max(raw, b.w.get(own, -1))
            if raw >= 0:
                deps[own] = raw
            else:
                del deps[own]
        return deps

    def _commit(self, token, reads, writes):
        k, v = token
        for b in reads:
            self._merge(b.r, k, v)
        for b in writes:
            if b.r:
                b.w = {}
                b.r = {}
            self._merge(b.w, k, v)

    def _take_barrier(self, eng, deps, keep_own=False):
        bd = self.pending_barrier[eng]
        if bd is not None:
            for k, v in bd.items():
                if k == ("e", eng) and not keep_own:
                    continue
                self._merge(deps, k, v)
            self.pending_barrier[eng] = None

    def op(self, eng, fn, reads=(), writes=()):
        deps = self._collect(eng, reads, writes)
        self._take_barrier(eng, deps)
        seq = len(self.ops[eng])
        self.ops[eng].append([fn, deps, None])
        self._commit((("e", eng), seq), reads, writes)

    def new_chan(self):
        self.chan_cnt.append(0)
        self.nchan += 1
        return self.nchan - 1

    def dma(self, eng, out, in_, reads=(), writes=(), cb=None):
        b = cb if cb is not None else (list(writes) + list(reads))[0]
        if b.chan is None:
            if b.name not in self.chan_by_name:
                self.chan_by_name[b.name] = self.new_chan()
            b.chan = self.chan_by_name[b.name]
        chan = b.chan
        deps = self._collect("dma", reads, writes)
        self._take_barrier(eng, deps, keep_own=True)
        self.chan_cnt[chan] += 16
        val = self.chan_cnt[chan]

        def fn(e, out=out, in_=in_):
            return e.dma_start(out=out, in_=in_)
        self.ops[eng].append([fn, deps, (chan, val)])
        self._commit((("d", chan), val), reads, writes)

    def barrier(self):
        bd = {}
        for e in ENGS:
            if self.ops[e]:
                for i in range(len(self.ops[e]) - 1, -1, -1):
                    if self.ops[e][i][2] is None:
                        bd[("e", e)] = i
                        break
        for c in range(self.nchan):
            if self.chan_cnt[c]:
                bd[("d", c)] = self.chan_cnt[c]
        for e in ENGS:
            old = self.pending_barrier[e]
            if old is not None:
                for k, v in old.items():
                    self._merge(bd, k, v)
            self.pending_barrier[e] = dict(bd)

    def emit(self, final_bufs=()):
        nc = self.nc
        fdeps = {}
        for b in final_bufs:
            for k, v in b.w.items():
                self._merge(fdeps, k, v)
        for e in ENGS:
            for fn, deps, tok in self.ops[e]:
                for (kind, key), v in deps.items():
                    if kind == "e":
                        self.miles[key].add(v)
        for (kind, key), v in fdeps.items():
            if kind == "e":
                self.miles[key].add(v)
        mile_idx = {}
        for e in ENGS:
            for i, s in enumerate(sorted(self.miles[e])):
                mile_idx[(e, s)] = i + 1
        with ExitStack() as st:
            esem = {e: st.enter_context(nc.semaphore("sem_" + e)) for e in ENGS}
            csem = [st.enter_context(nc.semaphore("dsem%d" % i)) for i in range(self.nchan)]
            block = st.enter_context(nc.Block())

            def run(e, engobj):
                known = {}
                for seq, (fn, deps, tok) in enumerate(self.ops[e]):
                    for (kind, key), v in deps.items():
                        if kind == "e":
                            need = mile_idx[(key, v)]
                            sem = esem[key]
                        else:
                            need = v
                            sem = csem[key]
                        if known.get((kind, key), 0) < need:
                            engobj.wait_ge(sem, need)
                            known[(kind, key)] = need
                    ins = fn(engobj)
                    if tok is not None:
                        ins.then_inc(csem[tok[0]], 16)
                    elif (e, seq) in mile_idx:
                        ins.then_inc(esem[e], 1)
                if e == "sp":
                    for (kind, key), v in fdeps.items():
                        if kind == "e":
                            engobj.wait_ge(esem[key], mile_idx[(key, v)])
                        else:
                            engobj.wait_ge(csem[key], v)

            block.tensor(lambda eng: run("pe", eng))
            block.scalar(lambda eng: run("act", eng))
            block.vector(lambda eng: run("dve", eng))
            block.gpsimd(lambda eng: run("pool", eng))
            block.sync(lambda eng: run("sp", eng))


class Ring:
    def __init__(self, items):
        self.items = items
        self.i = 0

    def next(self):
        it = self.items[self.i % len(self.items)]
        self.i += 1
        return it


class Arena:
    def __init__(self, raw, nwords):
        self.raw = raw
        self.n = nwords
        self.off = 0

    def reset(self):
        self.off = 0

    def alloc(self, free_elems, dt):
        words = free_elems if dt != BF16 else (free_elems + 1) // 2
        words = (words + 7) // 8 * 8
        assert self.off + words <= self.n, ("SBUF arena overflow", self.off, words, self.n)
        v = self.raw[:, self.off:self.off + words]
        self.off += words
        if dt == BF16:
            v = v.bitcast(BF16)[:, :free_elems]
        elif dt == I32:
            v = v.bitcast(I32)[:, :free_elems]
        else:
            v = v[:, :free_elems]
        return v


def build_program(T, depth):
    NTILE = T // TT
    NB = T // 128
    nc = bass.Bass("TRN2", target_bir_lowering=False)
    din = lambda n, s, dt=F32: nc.dram_tensor(n, s, dt, kind="ExternalInput").ap()
    xT = din("xT", [D, T])
    pT = din("pT", [depth, PLE, T])
    posrow = din("posrow", [128, T], I32)
    poscol = din("poscol", [128, NB], I32)
    gains_d = din("gains", [depth, 128, 96])
    aprm_d = din("aprm", [depth, 128, 32])
    sinkrow_d = din("sinkrow", [depth, 4, 64, 512])
    consts_d = din("consts", [8, 128, 512])
    Wn = {}
    for nm, shp in (("f1g", [D, DFF]), ("f1u", [D, DFF]), ("f1d", [DFF, D]), ("w_in", [D, INC]),
                    ("wuq", [512, 1536]), ("wukv", [256, 2048]), ("w_o", [D, D]),
                    ("f2g", [D, DFF]), ("f2u", [D, DFF]), ("f2d", [DFF, D]),
                    ("pproj", [PLE, D]), ("pgate", [D, D])):
        Wn[nm] = din(nm, [depth] + shp)
    yT = nc.dram_tensor("yT", [D, T], F32, kind="ExternalOutput").ap()
    Hs = nc.dram_tensor("Hs", [D, T], F32).ap()
    Zs = nc.dram_tensor("Zs", [INC, T], F32).ap()
    VAs = nc.dram_tensor("VAs", [T, 256], F32).ap()
    Os = nc.dram_tensor("Os", [D, T], F32).ap()

    def slabs(K, N, ns):
        return (N + ns - 1) // ns, K // 128, ns
    wspec = {"f1g": (D, DFF, 512), "f1u": (D, DFF, 512), "f1d": (DFF, D, 128), "w_in": (D, INC, 512),
             "w_o": (D, D, 512), "f2g": (D, DFF, 512), "f2u": (D, DFF, 512), "f2d": (DFF, D, 128),
             "pproj": (PLE, D, 512), "pgate": (D, D, 512), "w_inv": (D, 256, 256)}
    wscr = {}
    for par in range(2):
        for nm, (K, N, ns) in wspec.items():
            nsl, kc, _ = slabs(K, N, ns)
            wscr[(nm, par)] = nc.dram_tensor("ws_%s_%d" % (nm, par), [nsl, 128, kc * ns], BF16).ap()

    NWORDS = 46 * 1024
    with ExitStack() as st:
        raw = st.enter_context(nc.sbuf_tensor("raw", [128, NWORDS], F32))
        psum = [st.enter_context(nc.psum_tensor("ps%d" % i, [128, 512], F32)) for i in range(8)]
        A = Arena(raw, NWORDS)
        S = Sched(nc)
        psb = [Buf("ps%d" % i) for i in range(8)]
        hsb = [Buf("Hs%d" % t) for t in range(NTILE)]
        zb = Buf("Zs"); vab = Buf("VAs"); osb = Buf("Os"); outb = Buf("out")
        wsb = {k: Buf("ws_%s_%d" % k) for k in wscr}

        ones32 = A.alloc(128, F32); onesb = A.alloc(128, BF16)
        gains = A.alloc(96, F32); aprm = A.alloc(32, F32)
        base_off = None
        cb_ = Buf("consts"); gb = Buf("gains"); apb = Buf("aprm")
        S.op("dve", lambda e: e.memset(ones32, 1.0), writes=[cb_])
        S.op("dve", lambda e: e.memset(onesb, 1.0), writes=[cb_])
        base_off = A.off

        def precast(l):
            par = l % 2
            for nm, (K, N, ns) in wspec.items():
                src = Wn["w_in"][l][:, 1280:1536] if nm == "w_inv" else Wn[nm][l]
                sv = src.rearrange("(k p) n -> p k n", p=128)
                nsl, kc, _ = slabs(K, N, ns)
                for s in range(nsl):
                    w = min(ns, N - s * ns)
                    dst = wscr[(nm, par)][s][:, :kc * w].rearrange("p (k n) -> p k n", n=w)
                    S.dma("pool", dst, sv[:, :, s * ns:s * ns + w], writes=[wsb[(nm, par)]], cb=wsb[(nm, par)])

        def token_phase(mode, l):
            front = mode in ("mid", "last")
            back = mode in ("first", "mid")
            lf = l - 1 if front else None
            A.off = base_off
            WSLOT = 8192
            h = A.alloc(DC * TT, F32).rearrange("p (c t) -> p c t", t=TT)
            nT = A.alloc(DC * TT, BF16).rearrange("p (c t) -> p c t", t=TT)
            big = A.alloc(FC * TT, BF16)
            wslots = [A.alloc(WSLOT, BF16) for _ in range(4)]
            tmp = [A.alloc(TT, F32) for _ in range(4)]
            stg = [A.alloc(TT, F32) for _ in range(3)]
            rstd = [A.alloc(TT, F32) for _ in range(2)]
            pTs = A.alloc(2 * TT, BF16).rearrange("p (c t) -> p c t", t=TT)
            hb = [Buf("h%d" % c) for c in range(DC)]
            nb = [Buf("n%d" % c) for c in range(DC)]
            bigb = [Buf("big%d" % c) for c in range(FC)]
            wring = Ring([(wslots[i], Buf("w%d" % i)) for i in range(4)])
            psring = Ring([(psum[i], psb[i]) for i in range(6)])
            ps_stat = [(psum[6], psb[6]), (psum[7], psb[7])]
            tmpring = Ring([(tmp[i], Buf("tmp%d" % i)) for i in range(4)])
            stgring = Ring([(stg[i], Buf("stg%d" % i)) for i in range(3)])
            rstdb = [Buf("rstd0"), Buf("rstd1")]
            pb = Buf("pTs")
            actT = big.rearrange("p (f t) -> p f t", t=TT)
            big32 = big.bitcast(F32).rearrange("p (f t) -> p f t", t=TT)

            def b32(c):
                return [bigb[2 * c], bigb[2 * c + 1]]

            def linear(nm, lw, K, N, rhs_fn, rhs_bufs, evac):
                _, _, ns = wspec[nm]
                kc = K // 128
                scr = wscr[(nm, lw % 2)]; scb = wsb[(nm, lw % 2)]
                for si, n0 in enumerate(range(0, N, ns)):
                    w = min(ns, N - n0)
                    slot, sbuf_ = wring.next()
                    S.dma("sp", slot[:, :kc * w], scr[si][:, :kc * w], reads=[scb], writes=[sbuf_])
                    sv = slot[:, :kc * w].rearrange("p (k n) -> p k n", n=w)
                    for m0 in range(0, w, 128):
                        mw = min(128, w - m0)
                        ps, pb_ = psring.next()
                        for k in range(kc):
                            S.op("pe", lambda e, ps=ps, sv=sv, k=k, m0=m0, mw=mw: e.matmul(
                                ps[:mw, :TT], sv[:, k, m0:m0 + mw], rhs_fn(k), start=(k == 0), stop=(k == kc - 1)),
                                reads=[sbuf_, rhs_bufs[k]], writes=[pb_])
                        evac((n0 + m0) // 128, ps, pb_, mw)

            def rms_stats(src_fn, src_bufs, chunks, n_feat, which):
                ps, pb_ = ps_stat[which]
                for i, c in enumerate(chunks):
                    t, tb = tmpring.next()
                    S.op("act", lambda e, t=t, c=c: e.activation(out=t, in_=src_fn(c), func=AF.Square),
                         reads=src_bufs(c), writes=[tb])
                    S.op("pe", lambda e, t=t, i=i: e.matmul(ps[:, :TT], ones32, t, start=(i == 0),
                                                            stop=(i == len(chunks) - 1)),
                         reads=[tb, cb_], writes=[pb_])
                t, tb = tmpring.next()
                S.op("act", lambda e, t=t: e.activation(out=t, in_=ps[:, :TT], func=AF.Sqrt,
                                                        scale=1.0 / n_feat, bias=EPS), reads=[pb_], writes=[tb])
                S.op("dve", lambda e, t=t: e.reciprocal(out=rstd[which], in_=t), reads=[tb], writes=[rstdb[which]])

            def rmsnorm_h(gcol):
                rms_stats(lambda c: h[:, c, :], lambda c: [hb[c]], list(range(DC)), D, 0)
                for c in range(DC):
                    S.op("dve", lambda e, c=c: e.scalar_tensor_tensor(
                        out=nT[:, c, :], in0=h[:, c, :], scalar=gains[:, gcol + c:gcol + c + 1], in1=rstd[0],
                        op0=ALU.mult, op1=ALU.mult), reads=[hb[c], rstdb[0], gb], writes=[nb[c]])

            def ffn(pre, lw, gcol):
                rmsnorm_h(gcol)
                sg_, su_ = wscr[(pre + "g", lw % 2)], wscr[(pre + "u", lw % 2)]
                sgb_, sub_ = wsb[(pre + "g", lw % 2)], wsb[(pre + "u", lw % 2)]
                for si, n0 in enumerate(range(0, DFF, 512)):
                    sg, sgb = wring.next()
                    su, sub = wring.next()
                    S.dma("sp", sg[:, :DC * 512], sg_[si], reads=[sgb_], writes=[sgb])
                    S.dma("sp", su[:, :DC * 512], su_[si], reads=[sub_], writes=[sub])
                    sgv = sg[:, :DC * 512].rearrange("p (k n) -> p k n", n=512)
                    suv = su[:, :DC * 512].rearrange("p (k n) -> p k n", n=512)
                    for m0 in range(0, 512, 128):
                        f = (n0 + m0) // 128
                        pg, pgb = psring.next()
                        pu, pub = psring.next()
                        for (pp, ppb, vv, vb) in ((pg, pgb, sgv, sgb), (pu, pub, suv, sub)):
                            for k in range(DC):
                                S.op("pe", lambda e, pp=pp, vv=vv, k=k, m0=m0: e.matmul(
                                    pp[:, :TT], vv[:, k, m0:m0 + 128], nT[:, k, :], start=(k == 0), stop=(k == DC - 1)),
                                    reads=[vb, nb[k]], writes=[ppb])
                        t, tb = tmpring.next()
                        S.op("act", lambda e, t=t, pg=pg: e.activation(out=t, in_=pg[:, :TT], func=AF.Silu),
                             reads=[pgb], writes=[tb])
                        S.op("dve", lambda e, t=t, pu=pu, f=f: e.tensor_tensor(
                            out=actT[:, f, :], in0=t, in1=pu[:, :TT], op=ALU.mult),
                            reads=[tb, pub], writes=[bigb[f]])

                def evac_down(m, ps, pb_, mw):
                    S.op("dve", lambda e, m=m, ps=ps: e.scalar_tensor_tensor(
                        out=h[:, m, :], in0=ps[:, :TT], scalar=0.5, in1=h[:, m, :], op0=ALU.mult, op1=ALU.add),
                        reads=[pb_, hb[m]], writes=[hb[m]])
                linear(pre + "d", lw, DFF, D, lambda k: actT[:, k, :], bigb, evac_down)

            if back:
                S.dma("pool", gains[:, 0:32], gains_d[l][:, 0:32], writes=[gb])
            if front:
                S.dma("pool", gains[:, 32:96], gains_d[lf][:, 32:96], writes=[gb])
            for ti in range(NTILE):
                t0 = ti * TT
                if front:
                    S.dma("sp", h, Hs.rearrange("(c p) t -> p c t", p=128)[:, :, t0:t0 + TT], reads=[hsb[ti]], writes=hb)
                    S.dma("sp", big32[:, :DC, :], Os.rearrange("(c p) t -> p c t", p=128)[:, :, t0:t0 + TT],
                          reads=[osb], writes=bigb[:2 * DC])
                    rms_stats(lambda c: big32[:, c, :], b32, list(range(0, 8)), 1024, 0)
                    rms_stats(lambda c: big32[:, c, :], b32, list(range(8, 16)), 1024, 1)
                    for c in range(DC):
                        S.op("dve", lambda e, c=c: e.scalar_tensor_tensor(
                            out=nT[:, c, :], in0=big32[:, c, :], scalar=gains[:, 32 + c:33 + c], in1=rstd[c // 8],
                            op0=ALU.mult, op1=ALU.mult), reads=b32(c) + [rstdb[c // 8], gb], writes=[nb[c]])

                    def evac_wo(m, ps, pb_, mw):
                        S.op("dve", lambda e, m=m, ps=ps: e.tensor_tensor(
                            out=h[:, m, :], in0=ps[:, :TT], in1=h[:, m, :], op=ALU.add),
                            reads=[pb_, hb[m]], writes=[hb[m]])
                    linear("w_o", lf, D, D, lambda k: nT[:, k, :], nb, evac_wo)
                    ffn("f2", lf, 48)
                    S.dma("pool", pTs, pT[lf].rearrange("(c p) t -> p c t", p=128)[:, :, t0:t0 + TT], writes=[pb])
                    PP = big32

                    def evac_pp(m, ps, pb_, mw):
                        S.op("act", lambda e, m=m, ps=ps: e.activation(out=PP[:, m, :], in_=ps[:, :TT], func=AF.Copy),
                             reads=[pb_], writes=b32(m))
                    linear("pproj", lf, PLE, D, lambda k: pTs[:, k, :], [pb, pb], evac_pp)
                    rms_stats(lambda c: PP[:, c, :], b32, list(range(DC)), D, 1)
                    rmsnorm_h(64)

                    def evac_gate(m, ps, pb_, mw):
                        t, tb = tmpring.next()
                        S.op("act", lambda e, t=t, ps=ps: e.activation(out=t, in_=ps[:, :TT], func=AF.Sigmoid),
                             reads=[pb_], writes=[tb])
                        t2, t2b = tmpring.next()
                        S.op("dve", lambda e, t2=t2, m=m: e.scalar_tensor_tensor(
                            out=t2, in0=PP[:, m, :], scalar=gains[:, 80 + m:81 + m], in1=rstd[1],
                            op0=ALU.mult, op1=ALU.mult), reads=b32(m) + [rstdb[1], gb], writes=[t2b])
                        S.op("pool", lambda e, t=t, t2=t2: e.tensor_tensor(out=t2, in0=t2, in1=t, op=ALU.mult),
                             reads=[t2b, tb], writes=[t2b])
                        S.op("dve", lambda e, t2=t2, m=m: e.tensor_tensor(out=h[:, m, :], in0=h[:, m, :], in1=t2, op=ALU.add),
                             reads=[t2b, hb[m]], writes=[hb[m]])
                    linear("pgate", lf, D, D, lambda k: nT[:, k, :], nb, evac_gate)
                else:
                    S.dma("sp", h, xT.rearrange("(c p) t -> p c t", p=128)[:, :, t0:t0 + TT], writes=hb)
                if back:
                    ffn("f1", l, 0)
                    S.dma("sp", Hs.rearrange("(c p) t -> p c t", p=128)[:, :, t0:t0 + TT], h, reads=hb,
                          writes=[hsb[ti]], cb=hb[0])
                    rmsnorm_h(16)

                    def evac_z(m, ps, pb_, mw):
                        s_, s_b = stgring.next()
                        S.op("dve", lambda e, s_=s_, ps=ps, mw=mw: e.tensor_copy(out=s_[:mw, :], in_=ps[:mw, :TT]),
                             reads=[pb_], writes=[s_b])
                        S.dma("sp", Zs[m * 128:m * 128 + mw, t0:t0 + TT], s_[:mw, :], reads=[s_b], writes=[zb], cb=s_b)
                    linear("w_in", l, D, INC, lambda k: nT[:, k, :], nb, evac_z)
                    slot, sbuf_ = wring.next()
                    S.dma("sp", slot[:, :DC * 256], wscr[("w_inv", l % 2)][0], reads=[wsb[("w_inv", l % 2)]], writes=[sbuf_])
                    sv = slot[:, :DC * 256].rearrange("p (k n) -> p k n", n=256)
                    for tb_ in range(TT // 128):
                        ps, pb_ = psring.next()
                        for k in range(DC):
                            S.op("pe", lambda e, ps=ps, k=k, tb_=tb_, sv=sv: e.matmul(
                                ps[:, :256], nT[:, k, tb_ * 128:(tb_ + 1) * 128], sv[:, k, :], start=(k == 0),
                                stop=(k == DC - 1)), reads=[sbuf_, nb[k]], writes=[pb_])
                        s_, s_b = stgring.next()
                        S.op("dve", lambda e, s_=s_, ps=ps: e.tensor_copy(out=s_[:, :256], in_=ps[:, :256]),
                             reads=[pb_], writes=[s_b])
                        S.dma("sp", VAs[t0 + tb_ * 128:t0 + (tb_ + 1) * 128, :], s_[:, :256], reads=[s_b],
                              writes=[vab], cb=s_b)
                else:
                    S.dma("sp", yT.rearrange("(c p) t -> p c t", p=128)[:, :, t0:t0 + TT], h, reads=hb,
                          writes=[outb], cb=hb[0])

        def attn_phase(l):
            A.off = base_off
            par = l % 2
            cst = A.alloc(7 * 512, BF16).rearrange("p (m q) -> p m q", q=512)
            rotT = A.alloc(64, F32)
            posF = A.alloc(512, F32)
            posK = A.alloc(NB, F32)
            tmpi = A.alloc(512, I32)
            wuq = A.alloc(4 * 1536, BF16).rearrange("p (k n) -> p k n", n=1536)
            wukv = A.alloc(2 * 2048, BF16).rearrange("p (k n) -> p k n", n=2048)
            cstb = Buf("cst"); posb = Buf("pos"); wb = Buf("wattn"); tib = Buf("tmpi")
            S.dma("pool", aprm, aprm_d[l], writes=[apb])
            S.dma("pool", cst[:, 0:6, :], consts_d[0:6].rearrange("m p q -> p m q"), writes=[cstb])
            S.dma("sp", rotT[:64, :], consts_d[6][:64, :64], writes=[cstb])
            pkb_ = Buf("posK")
            S.dma("sp", tmpi[:, :NB], poscol[:, :], writes=[tib])
            S.op("dve", lambda e: e.tensor_copy(out=posK, in_=tmpi[:, :NB]), reads=[tib], writes=[pkb_])

            def load_pos(t0):
                S.dma("sp", tmpi, posrow[:, t0:t0 + TT], writes=[tib])
                S.op("dve", lambda e: e.tensor_copy(out=posF, in_=tmpi), reads=[tib], writes=[posb])
            S.dma("pool", wuq, Wn["wuq"][l].rearrange("(k p) n -> p k n", p=128), writes=[wb])
            S.dma("pool", wukv, Wn["wukv"][l].rearrange("(k p) n -> p k n", p=128), writes=[wb])
            mark = A.off

            def mk_tiles(n, nm):
                ts = [A.alloc(512, F32) for _ in range(n)]
                return Ring([(ts[i], Buf("%s%d" % (nm, i))) for i in range(n)])

            def stat_rstd(pieces, n_feat, rs, rsb, tring, pst):
                ps, pb_ = pst
                for i, (ap, rows, bufs) in enumerate(pieces):
                    t, tb = tring.next()
                    S.op("act", lambda e, t=t, ap=ap, rows=rows: e.activation(out=t[:rows, :], in_=ap, func=AF.Square),
                         reads=bufs, writes=[tb])
                    S.op("pe", lambda e, t=t, rows=rows, i=i: e.matmul(ps[:, :], ones32[:rows, :], t[:rows, :],
                                                                       start=(i == 0), stop=(i == len(pieces) - 1)),
                         reads=[tb, cb_], writes=[pb_])
                t, tb = tring.next()
                S.op("act", lambda e, t=t: e.activation(out=t, in_=ps[:, :], func=AF.Sqrt, scale=1.0 / n_feat, bias=EPS),
                     reads=[pb_], writes=[tb])
                S.op("dve", lambda e, t=t: e.reciprocal(out=rs, in_=t), reads=[tb], writes=[rsb])

            def swa(grp):
                A.off = mark
                KAp = A.alloc(T, BF16)
                VAp = A.alloc(NB * 64, BF16).rearrange("p (b d) -> p b d", d=64)
                esink = A.alloc(512, F32)
                qa32 = A.alloc(4 * 512, F32).rearrange("p (g t) -> p g t", t=512)
                ka32 = A.alloc(512, F32)
                qab = A.alloc(4 * 512, BF16).rearrange("p (b g q) -> p b g q", g=4, q=128)
                tring = mk_tiles(4, "st")
                sbr = mk_tiles(2, "sb")
                pfr = mk_tiles(2, "pf")
                pmt = [A.alloc(512, BF16) for _ in range(2)]
                pmr = Ring([(pmt[i], Buf("pm%d" % i)) for i in range(2)])
                dst = [A.alloc(128, F32) for _ in range(2)]
                dsr = Ring([(dst[i], Buf("ds%d" % i)) for i in range(2)])
                rs = A.alloc(512, F32); rsb = Buf("rs")
                ost = [A.alloc(512, F32) for _ in range(2)]
                osr = Ring([(ost[i], Buf("os%d" % i)) for i in range(2)])
                den = A.alloc(512, F32); denb = Buf("den")
                kab = Buf("KAp"); vapb = Buf("VAp"); esb = Buf("esink"); q32b = Buf("qa32"); k32b = Buf("ka32")
                qabb = Buf("qab")
                psS = Ring([(psum[i], psb[i]) for i in range(4)])
                pso = (psum[4], psb[4]); pssum = (psum[5], psb[5]); pst = (psum[6], psb[6])
                S.dma("sp", esink[:64, :], sinkrow_d[l][grp], writes=[esb])
                S.op("act", lambda e: e.activation(out=esink[:64, :], in_=esink[:64, :], func=AF.Exp), reads=[esb], writes=[esb])
                for ti in range(NTILE):
                    t0 = ti * TT
                    load_pos(t0)
                    S.dma("sp", qa32[:64], Zs[grp * 256:(grp + 1) * 256, t0:t0 + TT].rearrange("(g d) t -> d g t", d=64),
                          reads=[zb], writes=[q32b])
                    S.dma("sp", ka32[:64, :], Zs[1024 + grp * 64:1024 + (grp + 1) * 64, t0:t0 + TT], reads=[zb], writes=[k32b])
                    S.dma("pool", VAp[:, 4 * ti:4 * ti + 4, :],
                          VAs[t0:t0 + TT, grp * 64:(grp + 1) * 64].rearrange("(b p) d -> p b d", p=128),
                          reads=[vab], writes=[vapb])
                    for g in range(4):
                        stat_rstd([(qa32[:64, g, :], 64, [q32b])], 64, rs, rsb, tring, pst)
                        S.op("dve", lambda e, g=g: e.scalar_tensor_tensor(
                            out=qab[:64, :, g, :], in0=qa32[:64, g, :].rearrange("p (b q) -> p b q", q=128),
                            scalar=aprm[:64, 0:1], in1=rs[:64, :].rearrange("p (b q) -> p b q", q=128),
                            op0=ALU.mult, op1=ALU.mult), reads=[q32b, rsb, apb], writes=[qabb])
                    stat_rstd([(ka32[:64, :], 64, [k32b])], 64, rs, rsb, tring, pst)
                    S.op("dve", lambda e, t0=t0: e.scalar_tensor_tensor(
                        out=KAp[:64, t0:t0 + TT], in0=ka32[:64, :], scalar=aprm[:64, 1:2], in1=rs[:64, :],
                        op0=ALU.mult, op1=ALU.mult), reads=[k32b, rsb, apb], writes=[kab])
                    for qb in range(4):
                        Bq = 4 * ti + qb
                        kbs = [Bq - 1, Bq] if Bq > 0 else [Bq]
                        pms = []
                        for kbI in kbs:
                            ps, pb_ = psS.next()
                            S.op("pe", lambda e, ps=ps, kbI=kbI, qb=qb: e.matmul(
                                ps[:, :], KAp[:64, kbI * 128:(kbI + 1) * 128],
                                qab[:64, qb, :, :].rearrange("p g q -> p (g q)"), start=True, stop=True),
                                reads=[kab, qabb], writes=[pb_])
                            d_, d_b = dsr.next()
                            S.op("dve", lambda e, d_=d_, kbI=kbI, qb=qb: e.tensor_scalar(
                                out=d_, in0=posF[:, qb * 128:(qb + 1) * 128], scalar1=posK[:, kbI:kbI + 1], scalar2=None,
                                op0=ALU.subtract), reads=[posb, pkb_], writes=[d_b])
                            S.op("dve", lambda e, d_=d_: e.scalar_tensor_tensor(
                                out=d_, in0=d_, scalar=-1.0, in1=d_, op0=ALU.mult, op1=ALU.max), reads=[d_b], writes=[d_b])
                            sb_, sb_b = sbr.next()
                            for g in range(4):
                                S.op("dve", lambda e, sb_=sb_, d_=d_, ps=ps, g=g: e.scalar_tensor_tensor(
                                    out=sb_[:, g * 128:(g + 1) * 128], in0=d_, scalar=aprm[:, 13 + grp * 4 + g:14 + grp * 4 + g],
                                    in1=ps[:, g * 128:(g + 1) * 128], op0=ALU.mult, op1=ALU.add),
                                    reads=[d_b, pb_, apb], writes=[sb_b])
                            pf, pfb = pfr.next()
                            S.op("act", lambda e, pf=pf, sb_=sb_: e.activation(out=pf, in_=sb_, func=AF.Exp, scale=0.125),
                                 reads=[sb_b], writes=[pfb])
                            pm, pmb = pmr.next()
                            mi = 1 if kbI == Bq else 0
                            S.op("dve", lambda e, pm=pm, pf=pf, mi=mi: e.tensor_tensor(out=pm, in0=pf, in1=cst[:, mi, :], op=ALU.mult),
                                 reads=[pfb, cstb], writes=[pmb])
                            pms.append((pm, pmb, kbI))
                        for i, (pm, pmb, kbI) in enumerate(pms):
                            S.op("pe", lambda e, pm=pm, kbI=kbI, i=i, n=len(pms): e.matmul(
                                pso[0][:64, :], VAp[:, kbI, :], pm, start=(i == 0), stop=(i == n - 1)),
                                reads=[vapb, pmb], writes=[pso[1]])
                        for i, (pm, pmb, kbI) in enumerate(pms):
                            S.op("pe", lambda e, pm=pm, i=i, n=len(pms): e.matmul(
                                pssum[0][:64, :], onesb[:, :64], pm, start=(i == 0), stop=(i == n - 1)),
                                reads=[cb_, pmb], writes=[pssum[1]])
                        S.op("dve", lambda e: e.tensor_tensor(out=den[:64, :], in0=pssum[0][:64, :], in1=esink[:64, :], op=ALU.add),
                             reads=[pssum[1], esb], writes=[denb])
                        S.op("dve", lambda e: e.reciprocal(out=den[:64, :], in_=den[:64, :]), reads=[denb], writes=[denb])
                        o_, o_b = osr.next()
                        S.op("dve", lambda e, o_=o_: e.tensor_tensor(out=o_[:64, :], in0=pso[0][:64, :], in1=den[:64, :], op=ALU.mult),
                             reads=[pso[1], denb], writes=[o_b])
                        S.dma("sp", Os[grp * 256:(grp + 1) * 256, Bq * 128:(Bq + 1) * 128].rearrange("(g d) q -> d g q", d=64),
                              o_[:64, :].rearrange("p (g q) -> p g q", q=128), reads=[o_b], writes=[osb], cb=o_b)

            def mla(h0, nh):
                A.off = mark
                Kn = [A.alloc(T, BF16) for _ in range(nh)]
                Kr = [A.alloc(T, BF16) for _ in range(nh)]
                Vp = [A.alloc(NB * 128, BF16).rearrange("p (b d) -> p b d", d=128) for _ in range(nh)]
                knb = [Buf("Kn%d" % i) for i in range(nh)]; krb = [Buf("Kr%d" % i) for i in range(nh)]
                vpb = [Buf("Vp%d" % i) for i in range(nh)]
                cq32 = A.alloc(4 * 512, F32).rearrange("p (c t) -> p c t", t=512); cqb = Buf("cq32")
                ckv32 = A.alloc(2 * 512, F32).rearrange("p (c t) -> p c t", t=512); ckvb = Buf("ckv32")
                kr32 = A.alloc(512, F32); kr32b = Buf("kr32")
                cqn = A.alloc(4 * 512, BF16).rearrange("p (c t) -> p c t", t=512); cqnb = Buf("cqn")
                ckvn = A.alloc(2 * 512, BF16).rearrange("p (c t) -> p c t", t=512); ckvnb = Buf("ckvn")
                cos_ = A.alloc(512, F32); sin_ = A.alloc(512, F32); csb = Buf("cossin")
                ang = A.alloc(512, F32); angb = Buf("ang")
                ki32 = A.alloc(512, I32); kib = Buf("ki32")
                Qn = [A.alloc(512, BF16) for _ in range(nh)]; Qr = [A.alloc(512, BF16) for _ in range(nh)]
                qnb = [Buf("Qn%d" % i) for i in range(nh)]; qrb = [Buf("Qr%d" % i) for i in range(nh)]
                tring = mk_tiles(4, "mt")
                rs = A.alloc(512, F32); rsb = Buf("mrs")
                r32 = A.alloc(512, F32); r32b = Buf("r32")
                t1 = A.alloc(512, F32); t1b = Buf("t1"); t2 = A.alloc(512, F32); t2b = Buf("t2")
                pft = [A.alloc(512, F32) for _ in range(2)]
                pfr = Ring([(pft[i], Buf("mpf%d" % i)) for i in range(2)])
                pmt = [A.alloc(512, BF16) for _ in range(3)]
                pmr = Ring([(pmt[i], Buf("mpm%d" % i)) for i in range(3)])
                ost = [A.alloc(512, F32) for _ in range(2)]
                osr = Ring([(ost[i], Buf("mos%d" % i)) for i in range(2)])
                rec = A.alloc(512, F32); recb = Buf("rec")
                psS = Ring([(psum[i], psb[i]) for i in range(3)])
                psP = Ring([(psum[3], psb[3]), (psum[4], psb[4])])
                pso = (psum[5], psb[5]); pssum = (psum[6], psb[6]); pst = (psum[7], psb[7])
                scale = 192.0 ** -0.5

                def rope(src, srcb, dst_ap, dstb):
                    ps, pb_ = psP.next()
                    S.op("pe", lambda e, ps=ps: e.matmul(ps[:64, :], rotT[:64, :64], src[:64, :], start=True, stop=True),
                         reads=[srcb, cstb], writes=[pb_])
                    S.op("dve", lambda e: e.tensor_tensor(out=t1[:64, :], in0=src[:64, :], in1=cos_[:64, :], op=ALU.mult),
                         reads=[srcb, csb], writes=[t1b])
                    S.op("dve", lambda e, ps=ps: e.tensor_tensor(out=t2[:64, :], in0=ps[:64, :], in1=sin_[:64, :], op=ALU.mult),
                         reads=[pb_, csb], writes=[t2b])
                    S.op("dve", lambda e: e.tensor_tensor(out=dst_ap, in0=t1[:64, :], in1=t2[:64, :], op=ALU.add),
                         reads=[t1b, t2b], writes=[dstb])

                for ti in range(NTILE):
                    t0 = ti * TT
                    S.dma("sp", cq32, Zs[1536:2048, t0:t0 + TT].rearrange("(c p) t -> p c t", p=128), reads=[zb], writes=[cqb])
                    S.dma("sp", ckv32, Zs[2048:2304, t0:t0 + TT].rearrange("(c p) t -> p c t", p=128), reads=[zb], writes=[ckvb])
                    S.dma("sp", kr32[:64, :], Zs[2304:2368, t0:t0 + TT], reads=[zb], writes=[kr32b])
                    load_pos(t0)
                    C1 = 6.28125
                    C2 = 2 * PI - 6.28125
                    BND = 3.141592
                    S.op("dve", lambda e: e.tensor_scalar(out=ang[:64, :], in0=posF[:64, :], scalar1=aprm[:64, 12:13],
                                                          scalar2=None, op0=ALU.mult), reads=[posb, apb], writes=[angb])
                    S.op("dve", lambda e: e.tensor_scalar(out=t1[:64, :], in0=ang[:64, :], scalar1=1.0 / (2 * PI), scalar2=None,
                                                          op0=ALU.mult), reads=[angb], writes=[t1b])
                    S.op("dve", lambda e: e.tensor_copy(out=ki32[:64, :], in_=t1[:64, :]), reads=[t1b], writes=[kib])
                    S.op("dve", lambda e: e.tensor_copy(out=t1[:64, :], in_=ki32[:64, :]), reads=[kib], writes=[t1b])
                    S.op("dve", lambda e: e.scalar_tensor_tensor(out=t2[:64, :], in0=t1[:64, :], scalar=-C1, in1=ang[:64, :],
                                                                 op0=ALU.mult, op1=ALU.add), reads=[t1b, angb], writes=[t2b])
                    S.op("dve", lambda e: e.scalar_tensor_tensor(out=ang[:64, :], in0=t1[:64, :], scalar=-C2, in1=t2[:64, :],
                                                                 op0=ALU.mult, op1=ALU.add), reads=[t1b, t2b], writes=[angb])
                    for which, dst in ((0, sin_), (1, cos_)):
                        if which == 1:
                            S.op("dve", lambda e: e.tensor_scalar(out=ang[:64, :], in0=ang[:64, :], scalar1=PI / 2, scalar2=None,
                                                                  op0=ALU.add), reads=[angb], writes=[angb])
                        S.op("dve", lambda e: e.tensor_scalar(out=t1[:64, :], in0=ang[:64, :], scalar1=PI, scalar2=-2 * PI,
                                                              op0=ALU.is_gt, op1=ALU.mult), reads=[angb], writes=[t1b])
                        S.op("dve", lambda e: e.tensor_tensor(out=t1[:64, :], in0=t1[:64, :], in1=ang[:64, :], op=ALU.add),
                             reads=[t1b, angb], writes=[t1b])
                        S.op("dve", lambda e: e.tensor_scalar(out=t1[:64, :], in0=t1[:64, :], scalar1=-BND, scalar2=BND,
                                                              op0=ALU.max, op1=ALU.min), reads=[t1b], writes=[t1b])
                        S.op("act", lambda e, dst=dst: e.activation(out=dst[:64, :], in_=t1[:64, :], func=AF.Sin),
                             reads=[t1b], writes=[csb])
                    stat_rstd([(cq32[:, c, :], 128, [cqb]) for c in range(4)], 512, rs, rsb, tring, pst)
                    for c in range(4):
                        S.op("dve", lambda e, c=c: e.scalar_tensor_tensor(
                            out=cqn[:, c, :], in0=cq32[:, c, :], scalar=aprm[:, 2 + c:3 + c], in1=rs, op0=ALU.mult, op1=ALU.mult),
                            reads=[cqb, rsb, apb], writes=[cqnb])
                    stat_rstd([(ckv32[:, c, :], 128, [ckvb]) for c in range(2)], 256, rs, rsb, tring, pst)
                    for c in range(2):
                        S.op("dve", lambda e, c=c: e.scalar_tensor_tensor(
                            out=ckvn[:, c, :], in0=ckv32[:, c, :], scalar=aprm[:, 6 + c:7 + c], in1=rs, op0=ALU.mult, op1=ALU.mult),
                            reads=[ckvb, rsb, apb], writes=[ckvnb])
                    for hh in range(nh):
                        hd = h0 + hh
                        pq, pqb = psP.next()
                        for c in range(4):
                            S.op("pe", lambda e, pq=pq, c=c, hd=hd: e.matmul(pq[:, :], wuq[:, c, hd * 192:hd * 192 + 128], cqn[:, c, :],
                                                                             start=(c == 0), stop=(c == 3)), reads=[wb, cqnb], writes=[pqb])
                        pr, prb = psP.next()
                        for c in range(4):
                            S.op("pe", lambda e, pr=pr, c=c, hd=hd: e.matmul(pr[:64, :], wuq[:, c, hd * 192 + 128:hd * 192 + 192], cqn[:, c, :],
                                                                             start=(c == 0), stop=(c == 3)), reads=[wb, cqnb], writes=[prb])
                        stat_rstd([(pq[:, :], 128, [pqb]), (pr[:64, :], 64, [prb])], 192, rs, rsb, tring, pst)
                        S.op("dve", lambda e, pq=pq, hh=hh: e.scalar_tensor_tensor(
                            out=Qn[hh], in0=pq[:, :], scalar=aprm[:, 8:9], in1=rs, op0=ALU.mult, op1=ALU.mult),
                            reads=[pqb, rsb, apb], writes=[qnb[hh]])
                        S.op("dve", lambda e, pr=pr: e.scalar_tensor_tensor(
                            out=r32[:64, :], in0=pr[:64, :], scalar=aprm[:64, 9:10], in1=rs[:64, :], op0=ALU.mult, op1=ALU.mult),
                            reads=[prb, rsb, apb], writes=[r32b])
                        rope(r32, r32b, Qr[hh][:64, :], qrb[hh])
                        pk, pkb = psP.next()
                        for c in range(2):
                            S.op("pe", lambda e, pk=pk, c=c, hd=hd: e.matmul(pk[:, :], wukv[:, c, hd * 256:hd * 256 + 128], ckvn[:, c, :],
                                                                             start=(c == 0), stop=(c == 1)), reads=[wb, ckvnb], writes=[pkb])
                        stat_rstd([(pk[:, :], 128, [pkb]), (kr32[:64, :], 64, [kr32b])], 192, rs, rsb, tring, pst)
                        S.op("dve", lambda e, pk=pk, hh=hh, t0=t0: e.scalar_tensor_tensor(
                            out=Kn[hh][:, t0:t0 + TT], in0=pk[:, :], scalar=aprm[:, 10:11], in1=rs, op0=ALU.mult, op1=ALU.mult),
                            reads=[pkb, rsb, apb], writes=[knb[hh]])
                        S.op("dve", lambda e: e.scalar_tensor_tensor(
                            out=r32[:64, :], in0=kr32[:64, :], scalar=aprm[:64, 11:12], in1=rs[:64, :], op0=ALU.mult, op1=ALU.mult),
                            reads=[kr32b, rsb, apb], writes=[r32b])
                        rope(r32, r32b, Kr[hh][:64, t0:t0 + TT], krb[hh])
                        for tb_ in range(4):
                            pv, pvb = psP.next()
                            for c in range(2):
                                S.op("pe", lambda e, pv=pv, c=c, hd=hd, tb_=tb_: e.matmul(
                                    pv[:, :128], ckvn[:, c, tb_ * 128:(tb_ + 1) * 128], wukv[:, c, hd * 256 + 128:hd * 256 + 256],
                                    start=(c == 0), stop=(c == 1)), reads=[wb, ckvnb], writes=[pvb])
                            S.op("act", lambda e, pv=pv, hh=hh, tb_=tb_, ti=ti: e.activation(
                                out=Vp[hh][:, 4 * ti + tb_, :], in_=pv[:, :128], func=AF.Copy), reads=[pvb], writes=[vpb[hh]])
                    for hh in range(nh):
                        hd = h0 + hh
                        nkb = 4 * (ti + 1)
                        for kb in range(nkb):
                            ps, pb_ = psS.next()
                            S.op("pe", lambda e, ps=ps, kb=kb, hh=hh: e.matmul(ps[:, :], Kn[hh][:, kb * 128:(kb + 1) * 128], Qn[hh],
                                                                                start=True, stop=False), reads=[knb[hh], qnb[hh]], writes=[pb_])
                            S.op("pe", lambda e, ps=ps, kb=kb, hh=hh: e.matmul(ps[:, :], Kr[hh][:64, kb * 128:(kb + 1) * 128], Qr[hh][:64, :],
                                                                                start=False, stop=True), reads=[krb[hh], qrb[hh]], writes=[pb_])
                            pm, pmb = pmr.next()
                            if kb >= 4 * ti:
                                pf, pfb = pfr.next()
                                S.op("act", lambda e, pf=pf, ps=ps: e.activation(out=pf, in_=ps[:, :], func=AF.Exp, scale=scale),
                                     reads=[pb_], writes=[pfb])
                                S.op("dve", lambda e, pm=pm, pf=pf, kb=kb, ti=ti: e.tensor_tensor(
                                    out=pm, in0=pf, in1=cst[:, 2 + kb - 4 * ti, :], op=ALU.mult), reads=[pfb, cstb], writes=[pmb])
                            else:
                                S.op("act", lambda e, pm=pm, ps=ps: e.activation(out=pm, in_=ps[:, :], func=AF.Exp, scale=scale),
                                     reads=[pb_], writes=[pmb])
                            S.op("pe", lambda e, pm=pm, kb=kb, hh=hh, nkb=nkb: e.matmul(pso[0][:, :], Vp[hh][:, kb, :], pm,
                                                                                        start=(kb == 0), stop=(kb == nkb - 1)),
                                 reads=[vpb[hh], pmb], writes=[pso[1]])
                            S.op("pe", lambda e, pm=pm, kb=kb, nkb=nkb: e.matmul(pssum[0][:, :], onesb, pm,
                                                                                 start=(kb == 0), stop=(kb == nkb - 1)),
                                 reads=[cb_, pmb], writes=[pssum[1]])
                        S.op("dve", lambda e: e.reciprocal(out=rec, in_=pssum[0][:, :]), reads=[pssum[1]], writes=[recb])
                        o_, o_b = osr.next()
                        S.op("dve", lambda e, o_=o_: e.tensor_tensor(out=o_, in0=pso[0][:, :], in1=rec, op=ALU.mult),
                             reads=[pso[1], recb], writes=[o_b])
                        S.dma("sp", Os[1024 + hd * 128:1024 + (hd + 1) * 128, t0:t0 + TT], o_, reads=[o_b], writes=[osb], cb=o_b)

            for grp in range(4):
                swa(grp)
                S.barrier()
            if l + 1 < depth:
                precast(l + 1)
            for h0 in range(8):
                mla(h0, 1)
                S.barrier()

        precast(0)
        for l in range(depth):
            token_phase("first" if l == 0 else "mid", l)
            S.barrier()
            attn_phase(l)
        token_phase("last", depth)
        S.emit(final_bufs=[outb])
    return nc


def _lay(g):
    return np.ascontiguousarray(np.asarray(g, np.float32).reshape(-1, 128).T)


def _host_constants():
    k = np.arange(128)[:, None]; q = np.arange(128)[None, :]
    c = np.zeros((8, 128, 512), np.float32)
    c[0] = np.tile((k > q).astype(np.float32), (1, 4))
    c[1] = np.tile((k <= q).astype(np.float32), (1, 4))
    qq = np.arange(512)[None, :]
    for kb in range(4):
        c[2 + kb] = ((kb * 128 + k) <= qq).astype(np.float32)
    R = np.zeros((64, 64), np.float32)
    for m in range(32):
        R[m + 32, m] = -1.0
        R[m, m + 32] = 1.0
    c[6, :64, :64] = R
    return c


def kernel(x, p, positions, ffn1_norm, ffn1_w_gate, ffn1_w_up, ffn1_w_down, mix_norm, w_in, swa_q_norm,
           swa_k_norm, swa_sinks, mla_q_lora_norm, mla_w_uq, mla_kv_lora_norm, mla_w_ukv, mla_q_norm,
           mla_k_norm, out_norm_swa, out_norm_mla, w_o, ffn2_norm, ffn2_w_gate, ffn2_w_up, ffn2_w_down,
           ple_proj, ple_proj_norm, ple_gate_norm, ple_gate):
    x = np.asarray(x); p = np.asarray(p); positions = np.asarray(positions)
    B, T, _ = x.shape
    depth = p.shape[0]
    f = lambda a: np.ascontiguousarray(np.asarray(a, np.float32))
    gains = np.stack([np.concatenate([
        _lay(ffn1_norm[l]), _lay(mix_norm[l]),
        _lay(np.concatenate([np.asarray(out_norm_swa[l]), np.asarray(out_norm_mla[l])])),
        _lay(ffn2_norm[l]), _lay(ple_gate_norm[l]), _lay(ple_proj_norm[l])], axis=1) for l in range(depth)])
    half = 32
    inv_freq = (10000.0 ** (-np.arange(half, dtype=np.float32) / half)).astype(np.float32)
    slopes = (2.0 ** (-8.0 * np.arange(1, 17, dtype=np.float32) / 16)).astype(np.float32)
    aprm = np.zeros((depth, 128, 32), np.float32)
    for l in range(depth):
        a = aprm[l]
        a[:64, 0] = np.asarray(swa_q_norm[l]); a[:64, 1] = np.asarray(swa_k_norm[l])
        a[:, 2:6] = _lay(mla_q_lora_norm[l]); a[:, 6:8] = _lay(mla_kv_lora_norm[l])
        qn = np.asarray(mla_q_norm[l]); kn = np.asarray(mla_k_norm[l])
        a[:, 8] = qn[:128]; a[:64, 9] = qn[128:]; a[:, 10] = kn[:128]; a[:64, 11] = kn[128:]
        a[:32, 12] = inv_freq; a[32:64, 12] = inv_freq
        a[:, 13:29] = (-8.0 * slopes)[None, :]
        a[:, 30] = np.float32(PI)
    consts = _host_constants()
    sinkrow = np.zeros((depth, 4, 64, 512), np.float32)
    for l in range(depth):
        s = np.asarray(swa_sinks[l], np.float32)
        for grp in range(4):
            sinkrow[l, grp] = np.repeat(s[grp * 4:(grp + 1) * 4], 128)[None, :]
    f32 = lambda a: np.ascontiguousarray(np.asarray(a, np.float32))
    wnames = dict(f1g=ffn1_w_gate, f1u=ffn1_w_up, f1d=ffn1_w_down, w_in=w_in, wuq=mla_w_uq, wukv=mla_w_ukv,
                  w_o=w_o, f2g=ffn2_w_gate, f2u=ffn2_w_up, f2d=ffn2_w_down, pproj=ple_proj, pgate=ple_gate)
    wnames = {k: np.asarray(v) for k, v in wnames.items()}
    nc = build_program(T, 1)
    cur = [np.ascontiguousarray(x[b].T) for b in range(B)]
    posrow = [np.ascontiguousarray(np.broadcast_to(positions[b][None, :], (128, T))).astype(np.int32) for b in range(B)]
    poscol = [np.ascontiguousarray(positions[b].reshape(T // 128, 128).T).astype(np.int32) for b in range(B)]
    for l in range(depth):
        shared = {k: f32(v[l:l + 1]) for k, v in wnames.items()}
        shared.update(gains=f32(gains[l:l + 1]), consts=f32(consts), sinkrow=f32(sinkrow[l:l + 1]), aprm=f32(aprm[l:l + 1]))
        in_maps = []
        for b in range(B):
            m = dict(shared)
            m["xT"] = cur[b]
            m["pT"] = np.ascontiguousarray(np.transpose(p[l:l + 1, b], (0, 2, 1)))
            m["posrow"] = posrow[b]
            m["poscol"] = poscol[b]
            in_maps.append(m)
        res = run_bass_kernel_spmd(nc, in_maps, core_ids=list(range(B)))
        cur = [np.ascontiguousarray(res.results[b]["yT"]) for b in range(B)]
    out = np.stack([np.ascontiguousarray(cur[b].T) for b in range(B)])
    return out.astype(np.float32)
```

```python
import math
from contextlib import ExitStack

import numpy as np
import concourse.bass as bass
import concourse.mybir as mybir
from concourse.bass_utils import run_bass_kernel_spmd

F32 = mybir.dt.float32
BF16 = mybir.dt.bfloat16
I32 = mybir.dt.int32
AF = mybir.ActivationFunctionType
ALU = mybir.AluOpType

ENGS = ("pe", "act", "dve", "pool", "sp")

D = 2048; DFF = 5632; INC = 2368; PLE = 256
DC = D // 128; FC = DFF // 128
EPS = 1e-6
DEPTH = 4; BATCH = 2; SEQ = 8192
TT = 512
PI = math.pi


class Buf:
    __slots__ = ("name", "w", "r", "chan")

    def __init__(self, name):
        self.name = name
        self.w = {}
        self.r = {}
        self.chan = None


class Sched:
    def __init__(self, nc):
        self.nc = nc
        self.ops = {e: [] for e in ENGS}
        self.nchan = 0
        self.chan_cnt = []
        self.miles = {e: set() for e in ENGS}
        self.pending_barrier = {e: None for e in ENGS}
        self.chan_by_name = {}

    @staticmethod
    def _merge(d, key, val):
        if d.get(key, -1) < val:
            d[key] = val

    def _collect(self, eng, reads, writes):
        deps = {}
        for b in reads:
            for k, v in b.w.items():
                self._merge(deps, k, v)
        for b in writes:
            for k, v in b.w.items():
                self._merge(deps, k, v)
            for k, v in b.r.items():
                self._merge(deps, k, v)
        own = ("e", eng)
        if own in deps:
            raw = -1
            if eng != "pe":
                for b in reads:
                    raw = max(raw, b.w.get(own, -1))
            if raw >= 0:
                deps[own] = raw
            else:
                del deps[own]
        return deps

    def _commit(self, token, reads, writes):
        k, v = token
        for b in reads:
            self._merge(b.r, k, v)
        for b in writes:
            if b.r:
                b.w = {}
                b.r = {}
            self._merge(b.w, k, v)

    def _take_barrier(self, eng, deps, keep_own=False):
        bd = self.pending_barrier[eng]
        if bd is not None:
            for k, v in bd.items():
                if k == ("e", eng) and not keep_own:
                    continue
                self._merge(deps, k, v)
            self.pending_barrier[eng] = None

    def op(self, eng, fn, reads=(), writes=()):
        deps = self._collect(eng, reads, writes)
        self._take_barrier(eng, deps)
        seq = len(self.ops[eng])
        self.ops[eng].append([fn, deps, None])
        self._commit((("e", eng), seq), reads, writes)

    def new_chan(self):
        self.chan_cnt.append(0)
        self.nchan += 1
        return self.nchan - 1

    def dma(self, eng, out, in_, reads=(), writes=(), cb=None):
        b = cb if cb is not None else (list(writes) + list(reads))[0]
        if b.chan is None:
            if b.name not in self.chan_by_name:
                self.chan_by_name[b.name] = self.new_chan()
            b.chan = self.chan_by_name[b.name]
        chan = b.chan
        deps = self._collect("dma", reads, writes)
        self._take_barrier(eng, deps, keep_own=True)
        self.chan_cnt[chan] += 16
        val = self.chan_cnt[chan]

        def fn(e, out=out, in_=in_):
            return e.dma_start(out=out, in_=in_)
        self.ops[eng].append([fn, deps, (chan, val)])
        self._commit((("d", chan), val), reads, writes)

    def barrier(self):
        bd = {}
        for e in ENGS:
            if self.ops[e]:
                for i in range(len(self.ops[e]) - 1, -1, -1):
                    if self.ops[e][i][2] is None:
                        bd[("e", e)] = i
                        break
        for c in range(self.nchan):
            if self.chan_cnt[c]:
                bd[("d", c)] = self.chan_cnt[c]
        for e in ENGS:
            old = self.pending_barrier[e]
            if old is not None:
                for k, v in old.items():
                    self._merge(bd, k, v)
            self.pending_barrier[e] = dict(bd)

    def emit(self, final_bufs=()):
        nc = self.nc
        fdeps = {}
        for b in final_bufs:
            for k, v in b.w.items():
                self._merge(fdeps, k, v)
        for e in ENGS:
            for fn, deps, tok in self.ops[e]:
                for (kind, key), v in deps.items():
                    if kind == "e":
                        self.miles[key].add(v)
        for (kind, key), v in fdeps.items():
            if kind == "e":
                self.miles[key].add(v)
        mile_idx = {}
        for e in ENGS:
            for i, s in enumerate(sorted(self.miles[e])):
                mile_idx[(e, s)] = i + 1
        with ExitStack() as st:
            esem = {e: st.enter_context(nc.semaphore("sem_" + e)) for e in ENGS}
            csem = [st.enter_context(nc.semaphore("dsem%d" % i)) for i in range(self.nchan)]
            block = st.enter_context(nc.Block())

            def run(e, engobj):
                known = {}
                for seq, (fn, deps, tok) in enumerate(self.ops[e]):
                    waits = []
                    for (kind, key), v in deps.items():
                        if kind == "e":
                            need = mile_idx[(key, v)]
                            sem = esem[key]
                        else:
                            need = v
                            sem = csem[key]
                        if known.get((kind, key), 0) < need:
                            waits.append((sem, need))
                            known[(kind, key)] = need
                    for sem, need in waits[:-1]:
                        engobj.wait_ge(sem, need)
                    ins = fn(engobj)
                    if waits:
                        ins._wait_ge(waits[-1][0], waits[-1][1])
                    if tok is not None:
                        ins.then_inc(csem[tok[0]], 16)
                    elif (e, seq) in mile_idx:
                        ins.then_inc(esem[e], 1)
                if e == "sp":
                    for (kind, key), v in fdeps.items():
                        if kind == "e":
                            engobj.wait_ge(esem[key], mile_idx[(key, v)])
                        else:
                            engobj.wait_ge(csem[key], v)

            block.tensor(lambda eng: run("pe", eng))
            block.scalar(lambda eng: run("act", eng))
            block.vector(lambda eng: run("dve", eng))
            block.gpsimd(lambda eng: run("pool", eng))
            block.sync(lambda eng: run("sp", eng))


class Ring:
    def __init__(self, items):
        self.items = items
        self.i = 0

    def next(self):
        it = self.items[self.i % len(self.items)]
        self.i += 1
        return it


class Arena:
    def __init__(self, raw, nwords):
        self.raw = raw
        self.n = nwords
        self.off = 0

    def reset(self):
        self.off = 0

    def alloc(self, free_elems, dt):
        words = free_elems if dt != BF16 else (free_elems + 1) // 2
        words = (words + 7) // 8 * 8
        assert self.off + words <= self.n, ("SBUF arena overflow", self.off, words, self.n)
        v = self.raw[:, self.off:self.off + words]
        self.off += words
        if dt == BF16:
            v = v.bitcast(BF16)[:, :free_elems]
        elif dt == I32:
            v = v.bitcast(I32)[:, :free_elems]
        else:
            v = v[:, :free_elems]
        return v


def build_program(T, depth):
    NTILE = T // TT
    NB = T // 128
    nc = bass.Bass("TRN2", target_bir_lowering=False)
    din = lambda n, s, dt=F32: nc.dram_tensor(n, s, dt, kind="ExternalInput").ap()
    xT = din("xT", [D, T])
    pT = din("pT", [depth, PLE, T])
    posrow = din("posrow", [128, T], I32)
    poscol = din("poscol", [128, NB], I32)
    gains_d = din("gains", [depth, 128, 96])
    aprm_d = din("aprm", [depth, 128, 32])
    sinkrow_d = din("sinkrow", [depth, 4, 64, 512])
    consts_d = din("consts", [8, 128, 512])
    Wn = {}
    for nm, shp in (("f1g", [D, DFF]), ("f1u", [D, DFF]), ("f1d", [DFF, D]), ("w_in", [D, INC]),
                    ("wuq", [512, 1536]), ("wukv", [256, 2048]), ("w_o", [D, D]),
                    ("f2g", [D, DFF]), ("f2u", [D, DFF]), ("f2d", [DFF, D]),
                    ("pproj", [PLE, D]), ("pgate", [D, D])):
        Wn[nm] = din(nm, [depth] + shp)
    yT = nc.dram_tensor("yT", [D, T], F32, kind="ExternalOutput").ap()
    Hs = nc.dram_tensor("Hs", [D, T], F32).ap()
    Zs = nc.dram_tensor("Zs", [INC, T], F32).ap()
    VAs = nc.dram_tensor("VAs", [T, 256], F32).ap()
    Os = nc.dram_tensor("Os", [D, T], F32).ap()

    def slabs(K, N, ns):
        return (N + ns - 1) // ns, K // 128, ns
    wspec = {"f1g": (D, DFF, 512), "f1u": (D, DFF, 512), "f1d": (DFF, D, 128), "w_in": (D, INC, 512),
             "w_o": (D, D, 512), "f2g": (D, DFF, 512), "f2u": (D, DFF, 512), "f2d": (DFF, D, 128),
             "pproj": (PLE, D, 512), "pgate": (D, D, 512), "w_inv": (D, 256, 256)}
    wscr = {}
    for par in range(2):
        for nm, (K, N, ns) in wspec.items():
            nsl, kc, _ = slabs(K, N, ns)
            wscr[(nm, par)] = nc.dram_tensor("ws_%s_%d" % (nm, par), [nsl, 128, kc * ns], BF16).ap()

    NWORDS = 46 * 1024
    with ExitStack() as st:
        raw = st.enter_context(nc.sbuf_tensor("raw", [128, NWORDS], F32))
        psum = [st.enter_context(nc.psum_tensor("ps%d" % i, [128, 512], F32)) for i in range(8)]
        A = Arena(raw, NWORDS)
        S = Sched(nc)
        psb = [Buf("ps%d" % i) for i in range(8)]
        hsb = [Buf("Hs%d" % t) for t in range(NTILE)]
        zb = Buf("Zs"); vab = Buf("VAs"); osb = Buf("Os"); outb = Buf("out")
        wsb = {k: Buf("ws_%s_%d" % k) for k in wscr}

        ones32 = A.alloc(128, F32); onesb = A.alloc(128, BF16)
        gains = A.alloc(96, F32); aprm = A.alloc(32, F32)
        base_off = None
        cb_ = Buf("consts"); gb = Buf("gains"); apb = Buf("aprm")
        S.op("dve", lambda e: e.memset(ones32, 1.0), writes=[cb_])
        S.op("dve", lambda e: e.memset(onesb, 1.0), writes=[cb_])
        base_off = A.off

        def precast(l):
            par = l % 2
            for nm, (K, N, ns) in wspec.items():
                src = Wn["w_in"][l][:, 1280:1536] if nm == "w_inv" else Wn[nm][l]
                sv = src.rearrange("(k p) n -> p k n", p=128)
                nsl, kc, _ = slabs(K, N, ns)
                for s in range(nsl):
                    w = min(ns, N - s * ns)
                    dst = wscr[(nm, par)][s][:, :kc * w].rearrange("p (k n) -> p k n", n=w)
                    S.dma("pool", dst, sv[:, :, s * ns:s * ns + w], writes=[wsb[(nm, par)]], cb=wsb[(nm, par)])

        def token_phase(mode, l):
            front = mode in ("mid", "last")
            back = mode in ("first", "mid")
            lf = l - 1 if front else None
            A.off = base_off
            WSLOT = 8192
            h = A.alloc(DC * TT, F32).rearrange("p (c t) -> p c t", t=TT)
            nT = A.alloc(DC * TT, BF16).rearrange("p (c t) -> p c t", t=TT)
            big = A.alloc(FC * TT, BF16)
            wslots = [A.alloc(WSLOT, BF16) for _ in range(4)]
            tmp = [A.alloc(TT, F32) for _ in range(4)]
            stg = [A.alloc(TT, F32) for _ in range(3)]
            rstd = [A.alloc(TT, F32) for _ in range(2)]
            pTs = A.alloc(2 * TT, BF16).rearrange("p (c t) -> p c t", t=TT)
            hb = [Buf("h%d" % c) for c in range(DC)]
            nb = [Buf("n%d" % c) for c in range(DC)]
            bigb = [Buf("big%d" % c) for c in range(FC)]
            wring = Ring([(wslots[i], Buf("w%d" % i)) for i in range(4)])
            psring = Ring([(psum[i], psb[i]) for i in range(6)])
            ps_stat = [(psum[6], psb[6]), (psum[7], psb[7])]
            tmpring = Ring([(tmp[i], Buf("tmp%d" % i)) for i in range(4)])
            stgring = Ring([(stg[i], Buf("stg%d" % i)) for i in range(3)])
            rstdb = [Buf("rstd0"), Buf("rstd1")]
            pb = Buf("pTs")
            actT = big.rearrange("p (f t) -> p f t", t=TT)
            big32 = big.bitcast(F32).rearrange("p (f t) -> p f t", t=TT)

            def b32(c):
                return [bigb[2 * c], bigb[2 * c + 1]]

            def linear(nm, lw, K, N, rhs_fn, rhs_bufs, evac):
                _, _, ns = wspec[nm]
                kc = K // 128
                scr = wscr[(nm, lw % 2)]; scb = wsb[(nm, lw % 2)]
                for si, n0 in enumerate(range(0, N, ns)):
                    w = min(ns, N - n0)
                    slot, sbuf_ = wring.next()
                    S.dma("sp", slot[:, :kc * w], scr[si][:, :kc * w], reads=[scb], writes=[sbuf_])
                    sv = slot[:, :kc * w].rearrange("p (k n) -> p k n", n=w)
                    for m0 in range(0, w, 128):
                        mw = min(128, w - m0)
                        ps, pb_ = psring.next()
                        for k in range(kc):
                            S.op("pe", lambda e, ps=ps, sv=sv, k=k, m0=m0, mw=mw: e.matmul(
                                ps[:mw, :TT], sv[:, k, m0:m0 + mw], rhs_fn(k), start=(k == 0), stop=(k == kc - 1)),
                                reads=[sbuf_, rhs_bufs[k]], writes=[pb_])
                        evac((n0 + m0) // 128, ps, pb_, mw)

            def rms_stats(src_fn, src_bufs, chunks, n_feat, which):
                ps, pb_ = ps_stat[which]
                for i, c in enumerate(chunks):
                    t, tb = tmpring.next()
                    S.op("act", lambda e, t=t, c=c: e.activation(out=t, in_=src_fn(c), func=AF.Square),
                         reads=src_bufs(c), writes=[tb])
                    S.op("pe", lambda e, t=t, i=i: e.matmul(ps[:, :TT], ones32, t, start=(i == 0),
                                                            stop=(i == len(chunks) - 1)),
                         reads=[tb, cb_], writes=[pb_])
                t, tb = tmpring.next()
                S.op("act", lambda e, t=t: e.activation(out=t, in_=ps[:, :TT], func=AF.Sqrt,
                                                        scale=1.0 / n_feat, bias=EPS), reads=[pb_], writes=[tb])
                S.op("dve", lambda e, t=t: e.reciprocal(out=rstd[which], in_=t), reads=[tb], writes=[rstdb[which]])

            def rmsnorm_h(gcol):
                rms_stats(lambda c: h[:, c, :], lambda c: [hb[c]], list(range(DC)), D, 0)
                for c in range(DC):
                    S.op("dve", lambda e, c=c: e.scalar_tensor_tensor(
                        out=nT[:, c, :], in0=h[:, c, :], scalar=gains[:, gcol + c:gcol + c + 1], in1=rstd[0],
                        op0=ALU.mult, op1=ALU.mult), reads=[hb[c], rstdb[0], gb], writes=[nb[c]])

            def ffn(pre, lw, gcol):
                rmsnorm_h(gcol)
                sg_, su_ = wscr[(pre + "g", lw % 2)], wscr[(pre + "u", lw % 2)]
                sgb_, sub_ = wsb[(pre + "g", lw % 2)], wsb[(pre + "u", lw % 2)]
                for si, n0 in enumerate(range(0, DFF, 512)):
                    sg, sgb = wring.next()
                    su, sub = wring.next()
                    S.dma("sp", sg[:, :DC * 512], sg_[si], reads=[sgb_], writes=[sgb])
                    S.dma("sp", su[:, :DC * 512], su_[si], reads=[sub_], writes=[sub])
                    sgv = sg[:, :DC * 512].rearrange("p (k n) -> p k n", n=512)
                    suv = su[:, :DC * 512].rearrange("p (k n) -> p k n", n=512)
                    for m0 in range(0, 512, 128):
                        f = (n0 + m0) // 128
                        pg, pgb = psring.next()
                        pu, pub = psring.next()
                        for (pp, ppb, vv, vb) in ((pg, pgb, sgv, sgb), (pu, pub, suv, sub)):
                            for k in range(DC):
                                S.op("pe", lambda e, pp=pp, vv=vv, k=k, m0=m0: e.matmul(
                                    pp[:, :TT], vv[:, k, m0:m0 + 128], nT[:, k, :], start=(k == 0), stop=(k == DC - 1)),
                                    reads=[vb, nb[k]], writes=[ppb])
                        t, tb = tmpring.next()
                        S.op("act", lambda e, t=t, pg=pg: e.activation(out=t, in_=pg[:, :TT], func=AF.Silu),
                             reads=[pgb], writes=[tb])
                        S.op("dve", lambda e, t=t, pu=pu, f=f: e.tensor_tensor(
                            out=actT[:, f, :], in0=t, in1=pu[:, :TT], op=ALU.mult),
                            reads=[tb, pub], writes=[bigb[f]])

                def evac_down(m, ps, pb_, mw):
                    S.op("dve", lambda e, m=m, ps=ps: e.scalar_tensor_tensor(
                        out=h[:, m, :], in0=ps[:, :TT], scalar=0.5, in1=h[:, m, :], op0=ALU.mult, op1=ALU.add),
                        reads=[pb_, hb[m]], writes=[hb[m]])
                linear(pre + "d", lw, DFF, D, lambda k: actT[:, k, :], bigb, evac_down)

            if back:
                S.dma("pool", gains[:, 0:32], gains_d[l][:, 0:32], writes=[gb])
            if front:
                S.dma("pool", gains[:, 32:96], gains_d[lf][:, 32:96], writes=[gb])
            for ti in range(NTILE):
                t0 = ti * TT
                if front:
                    S.dma("sp", h, Hs.rearrange("(c p) t -> p c t", p=128)[:, :, t0:t0 + TT], reads=[hsb[ti]], writes=hb)
                    S.dma("sp", big32[:, :DC, :], Os.rearrange("(c p) t -> p c t", p=128)[:, :, t0:t0 + TT],
                          reads=[osb], writes=bigb[:2 * DC])
                    rms_stats(lambda c: big32[:, c, :], b32, list(range(0, 8)), 1024, 0)
                    rms_stats(lambda c: big32[:, c, :], b32, list(range(8, 16)), 1024, 1)
                    for c in range(DC):
                        S.op("dve", lambda e, c=c: e.scalar_tensor_tensor(
                            out=nT[:, c, :], in0=big32[:, c, :], scalar=gains[:, 32 + c:33 + c], in1=rstd[c // 8],
                            op0=ALU.mult, op1=ALU.mult), reads=b32(c) + [rstdb[c // 8], gb], writes=[nb[c]])

                    def evac_wo(m, ps, pb_, mw):
                        S.op("dve", lambda e, m=m, ps=ps: e.tensor_tensor(
                            out=h[:, m, :], in0=ps[:, :TT], in1=h[:, m, :], op=ALU.add),
                            reads=[pb_, hb[m]], writes=[hb[m]])
                    linear("w_o", lf, D, D, lambda k: nT[:, k, :], nb, evac_wo)
                    ffn("f2", lf, 48)
                    S.dma("pool", pTs, pT[lf].rearrange("(c p) t -> p c t", p=128)[:, :, t0:t0 + TT], writes=[pb])
                    PP = big32

                    def evac_pp(m, ps, pb_, mw):
                        S.op("act", lambda e, m=m, ps=ps: e.activation(out=PP[:, m, :], in_=ps[:, :TT], func=AF.Copy),
                             reads=[pb_], writes=b32(m))
                    linear("pproj", lf, PLE, D, lambda k: pTs[:, k, :], [pb, pb], evac_pp)
                    rms_stats(lambda c: PP[:, c, :], b32, list(range(DC)), D, 1)
                    rmsnorm_h(64)

                    def evac_gate(m, ps, pb_, mw):
                        t, tb = tmpring.next()
                        S.op("act", lambda e, t=t, ps=ps: e.activation(out=t, in_=ps[:, :TT], func=AF.Sigmoid),
                             reads=[pb_], writes=[tb])
                        t2, t2b = tmpring.next()
                        S.op("dve", lambda e, t2=t2, m=m: e.scalar_tensor_tensor(
                            out=t2, in0=PP[:, m, :], scalar=gains[:, 80 + m:81 + m], in1=rstd[1],
                            op0=ALU.mult, op1=ALU.mult), reads=b32(m) + [rstdb[1], gb], writes=[t2b])
                        S.op("pool", lambda e, t=t, t2=t2: e.tensor_tensor(out=t2, in0=t2, in1=t, op=ALU.mult),
                             reads=[t2b, tb], writes=[t2b])
                        S.op("dve", lambda e, t2=t2, m=m: e.tensor_tensor(out=h[:, m, :], in0=h[:, m, :], in1=t2, op=ALU.add),
                             reads=[t2b, hb[m]], writes=[hb[m]])
                    linear("pgate", lf, D, D, lambda k: nT[:, k, :], nb, evac_gate)
                else:
                    S.dma("sp", h, xT.rearrange("(c p) t -> p c t", p=128)[:, :, t0:t0 + TT], writes=hb)
                if back:
                    ffn("f1", l, 0)
                    S.dma("sp", Hs.rearrange("(c p) t -> p c t", p=128)[:, :, t0:t0 + TT], h, reads=hb,
                          writes=[hsb[ti]], cb=hb[0])
                    rmsnorm_h(16)

                    def evac_z(m, ps, pb_, mw):
                        s_, s_b = stgring.next()
                        S.op("dve", lambda e, s_=s_, ps=ps, mw=mw: e.tensor_copy(out=s_[:mw, :], in_=ps[:mw, :TT]),
                             reads=[pb_], writes=[s_b])
                        S.dma("sp", Zs[m * 128:m * 128 + mw, t0:t0 + TT], s_[:mw, :], reads=[s_b], writes=[zb], cb=s_b)
                    linear("w_in", l, D, INC, lambda k: nT[:, k, :], nb, evac_z)
                    slot, sbuf_ = wring.next()
                    S.dma("sp", slot[:, :DC * 256], wscr[("w_inv", l % 2)][0], reads=[wsb[("w_inv", l % 2)]], writes=[sbuf_])
                    sv = slot[:, :DC * 256].rearrange("p (k n) -> p k n", n=256)
                    for tb_ in range(TT // 128):
                        ps, pb_ = psring.next()
                        for k in range(DC):
                            S.op("pe", lambda e, ps=ps, k=k, tb_=tb_, sv=sv: e.matmul(
                                ps[:, :256], nT[:, k, tb_ * 128:(tb_ + 1) * 128], sv[:, k, :], start=(k == 0),
                                stop=(k == DC - 1)), reads=[sbuf_, nb[k]], writes=[pb_])
                        s_, s_b = stgring.next()
                        S.op("dve", lambda e, s_=s_, ps=ps: e.tensor_copy(out=s_[:, :256], in_=ps[:, :256]),
                             reads=[pb_], writes=[s_b])
                        S.dma("sp", VAs[t0 + tb_ * 128:t0 + (tb_ + 1) * 128, :], s_[:, :256], reads=[s_b],
                              writes=[vab], cb=s_b)
                else:
                    S.dma("sp", yT.rearrange("(c p) t -> p c t", p=128)[:, :, t0:t0 + TT], h, reads=hb,
                          writes=[outb], cb=hb[0])

        def attn_phase(l):
            A.off = base_off
            par = l % 2
            cst = A.alloc(7 * 512, BF16).rearrange("p (m q) -> p m q", q=512)
            rotT = A.alloc(64, F32)
            posF = A.alloc(512, F32)
            posK = A.alloc(NB, F32)
            tmpi = A.alloc(512, I32)
            wuq = A.alloc(4 * 1536, BF16).rearrange("p (k n) -> p k n", n=1536)
            wukv = A.alloc(2 * 2048, BF16).rearrange("p (k n) -> p k n", n=2048)
            cstb = Buf("cst"); posb = Buf("pos"); wb = Buf("wattn"); tib = Buf("tmpi")
            S.dma("pool", aprm, aprm_d[l], writes=[apb])
            S.dma("pool", cst[:, 0:6, :], consts_d[0:6].rearrange("m p q -> p m q"), writes=[cstb])
            S.dma("sp", rotT[:64, :], consts_d[6][:64, :64], writes=[cstb])
            pkb_ = Buf("posK")
            S.dma("sp", tmpi[:, :NB], poscol[:, :], writes=[tib])
            S.op("dve", lambda e: e.tensor_copy(out=posK, in_=tmpi[:, :NB]), reads=[tib], writes=[pkb_])

            def load_pos(t0):
                S.dma("sp", tmpi, posrow[:, t0:t0 + TT], writes=[tib])
                S.op("dve", lambda e: e.tensor_copy(out=posF, in_=tmpi), reads=[tib], writes=[posb])
            S.dma("pool", wuq, Wn["wuq"][l].rearrange("(k p) n -> p k n", p=128), writes=[wb])
            S.dma("pool", wukv, Wn["wukv"][l].rearrange("(k p) n -> p k n", p=128), writes=[wb])
            mark = A.off

            def mk_tiles(n, nm):
                ts = [A.alloc(512, F32) for _ in range(n)]
                return Ring([(ts[i], Buf("%s%d" % (nm, i))) for i in range(n)])

            def stat_rstd(pieces, n_feat, rs, rsb, tring, pst):
                ps, pb_ = pst
                for i, (ap, rows, bufs) in enumerate(pieces):
                    t, tb = tring.next()
                    S.op("act", lambda e, t=t, ap=ap, rows=rows: e.activation(out=t[:rows, :], in_=ap, func=AF.Square),
                         reads=bufs, writes=[tb])
                    S.op("pe", lambda e, t=t, rows=rows, i=i: e.matmul(ps[:, :], ones32[:rows, :], t[:rows, :],
                                                                       start=(i == 0), stop=(i == len(pieces) - 1)),
                         reads=[tb, cb_], writes=[pb_])
                t, tb = tring.next()
                S.op("act", lambda e, t=t: e.activation(out=t, in_=ps[:, :], func=AF.Sqrt, scale=1.0 / n_feat, bias=EPS),
                     reads=[pb_], writes=[tb])
                S.op("dve", lambda e, t=t: e.reciprocal(out=rs, in_=t), reads=[tb], writes=[rsb])

            def swa(grp):
                A.off = mark
                KAp = A.alloc(T, BF16)
                VAp = A.alloc(NB * 64, BF16).rearrange("p (b d) -> p b d", d=64)
                esink = A.alloc(512, F32)
                qa32 = A.alloc(4 * 512, F32).rearrange("p (g t) -> p g t", t=512)
                ka32 = A.alloc(512, F32)
                qab = A.alloc(4 * 512, BF16).rearrange("p (b g q) -> p b g q", g=4, q=128)
                tring = mk_tiles(4, "st")
                sbr = mk_tiles(2, "sb")
                pfr = mk_tiles(2, "pf")
                pmt = [A.alloc(512, BF16) for _ in range(2)]
                pmr = Ring([(pmt[i], Buf("pm%d" % i)) for i in range(2)])
                dst = [A.alloc(128, F32) for _ in range(2)]
                dsr = Ring([(dst[i], Buf("ds%d" % i)) for i in range(2)])
                rs = A.alloc(512, F32); rsb = Buf("rs")
                ost = [A.alloc(512, F32) for _ in range(2)]
                osr = Ring([(ost[i], Buf("os%d" % i)) for i in range(2)])
                den = A.alloc(512, F32); denb = Buf("den")
                kab = Buf("KAp"); vapb = Buf("VAp"); esb = Buf("esink"); q32b = Buf("qa32"); k32b = Buf("ka32")
                qabb = Buf("qab")
                psS = Ring([(psum[i], psb[i]) for i in range(4)])
                pso = (psum[4], psb[4]); pssum = (psum[5], psb[5]); pst = (psum[6], psb[6])
                S.dma("sp", esink[:64, :], sinkrow_d[l][grp], writes=[esb])
                S.op("act", lambda e: e.activation(out=esink[:64, :], in_=esink[:64, :], func=AF.Exp), reads=[esb], writes=[esb])
                for ti in range(NTILE):
                    t0 = ti * TT
                    load_pos(t0)
                    S.dma("sp", qa32[:64], Zs[grp * 256:(grp + 1) * 256, t0:t0 + TT].rearrange("(g d) t -> d g t", d=64),
                          reads=[zb], writes=[q32b])
                    S.dma("sp", ka32[:64, :], Zs[1024 + grp * 64:1024 + (grp + 1) * 64, t0:t0 + TT], reads=[zb], writes=[k32b])
                    S.dma("pool", VAp[:, 4 * ti:4 * ti + 4, :],
                          VAs[t0:t0 + TT, grp * 64:(grp + 1) * 64].rearrange("(b p) d -> p b d", p=128),
                          reads=[vab], writes=[vapb])
                    for g in range(4):
                        stat_rstd([(qa32[:64, g, :], 64, [q32b])], 64, rs, rsb, tring, pst)
                        S.op("dve", lambda e, g=g: e.scalar_tensor_tensor(
                            out=qab[:64, :, g, :], in0=qa32[:64, g, :].rearrange("p (b q) -> p b q", q=128),
                            scalar=aprm[:64, 0:1], in1=rs[:64, :].rearrange("p (b q) -> p b q", q=128),
                            op0=ALU.mult, op1=ALU.mult), reads=[q32b, rsb, apb], writes=[qabb])
                    stat_rstd([(ka32[:64, :], 64, [k32b])], 64, rs, rsb, tring, pst)
                    S.op("dve", lambda e, t0=t0: e.scalar_tensor_tensor(
                        out=KAp[:64, t0:t0 + TT], in0=ka32[:64, :], scalar=aprm[:64, 1:2], in1=rs[:64, :],
                        op0=ALU.mult, op1=ALU.mult), reads=[k32b, rsb, apb], writes=[kab])
                    for qb in range(4):
                        Bq = 4 * ti + qb
                        kbs = [Bq - 1, Bq] if Bq > 0 else [Bq]
                        pms = []
                        for kbI in kbs:
                            ps, pb_ = psS.next()
                            S.op("pe", lambda e, ps=ps, kbI=kbI, qb=qb: e.matmul(
                                ps[:, :], KAp[:64, kbI * 128:(kbI + 1) * 128],
                                qab[:64, qb, :, :].rearrange("p g q -> p (g q)"), start=True, stop=True),
                                reads=[kab, qabb], writes=[pb_])
                            d_, d_b = dsr.next()
                            S.op("dve", lambda e, d_=d_, kbI=kbI, qb=qb: e.tensor_scalar(
                                out=d_, in0=posF[:, qb * 128:(qb + 1) * 128], scalar1=posK[:, kbI:kbI + 1], scalar2=None,
                                op0=ALU.subtract), reads=[posb, pkb_], writes=[d_b])
                            S.op("dve", lambda e, d_=d_: e.scalar_tensor_tensor(
                                out=d_, in0=d_, scalar=-1.0, in1=d_, op0=ALU.mult, op1=ALU.max), reads=[d_b], writes=[d_b])
                            sb_, sb_b = sbr.next()
                            for g in range(4):
                                S.op("dve", lambda e, sb_=sb_, d_=d_, ps=ps, g=g: e.scalar_tensor_tensor(
                                    out=sb_[:, g * 128:(g + 1) * 128], in0=d_, scalar=aprm[:, 13 + grp * 4 + g:14 + grp * 4 + g],
                                    in1=ps[:, g * 128:(g + 1) * 128], op0=ALU.mult, op1=ALU.add),
                                    reads=[d_b, pb_, apb], writes=[sb_b])
                            pf, pfb = pfr.next()
                            S.op("act", lambda e, pf=pf, sb_=sb_: e.activation(out=pf, in_=sb_, func=AF.Exp, scale=0.125),
                                 reads=[sb_b], writes=[pfb])
                            pm, pmb = pmr.next()
                            mi = 1 if kbI == Bq else 0
                            S.op("dve", lambda e, pm=pm, pf=pf, mi=mi: e.tensor_tensor(out=pm, in0=pf, in1=cst[:, mi, :], op=ALU.mult),
                                 reads=[pfb, cstb], writes=[pmb])
                            pms.append((pm, pmb, kbI))
                        for i, (pm, pmb, kbI) in enumerate(pms):
                            S.op("pe", lambda e, pm=pm, kbI=kbI, i=i, n=len(pms): e.matmul(
                                pso[0][:64, :], VAp[:, kbI, :], pm, start=(i == 0), stop=(i == n - 1)),
                                reads=[vapb, pmb], writes=[pso[1]])
                        for i, (pm, pmb, kbI) in enumerate(pms):
                            S.op("pe", lambda e, pm=pm, i=i, n=len(pms): e.matmul(
                                pssum[0][:64, :], onesb[:, :64], pm, start=(i == 0), stop=(i == n - 1)),
                                reads=[cb_, pmb], writes=[pssum[1]])
                        S.op("dve", lambda e: e.tensor_tensor(out=den[:64, :], in0=pssum[0][:64, :], in1=esink[:64, :], op=ALU.add),
                             reads=[pssum[1], esb], writes=[denb])
                        S.op("dve", lambda e: e.reciprocal(out=den[:64, :], in_=den[:64, :]), reads=[denb], writes=[denb])
                        o_, o_b = osr.next()
                        S.op("dve", lambda e, o_=o_: e.tensor_tensor(out=o_[:64, :], in0=pso[0][:64, :], in1=den[:64, :], op=ALU.mult),
                             reads=[pso[1], denb], writes=[o_b])
                        S.dma("sp", Os[grp * 256:(grp + 1) * 256, Bq * 128:(Bq + 1) * 128].rearrange("(g d) q -> d g q", d=64),
                              o_[:64, :].rearrange("p (g q) -> p g q", q=128), reads=[o_b], writes=[osb], cb=o_b)

            def mla(h0, nh):
                A.off = mark
                Kn = [A.alloc(T, BF16) for _ in range(nh)]
                Kr = [A.alloc(T, BF16) for _ in range(nh)]
                Vp = [A.alloc(NB * 128, BF16).rearrange("p (b d) -> p b d", d=128) for _ in range(nh)]
                knb = [Buf("Kn%d" % i) for i in range(nh)]; krb = [Buf("Kr%d" % i) for i in range(nh)]
                vpb = [Buf("Vp%d" % i) for i in range(nh)]
                cq32 = A.alloc(4 * 512, F32).rearrange("p (c t) -> p c t", t=512); cqb = Buf("cq32")
                ckv32 = A.alloc(2 * 512, F32).rearrange("p (c t) -> p c t", t=512); ckvb = Buf("ckv32")
                kr32 = A.alloc(512, F32); kr32b = Buf("kr32")
                cqn = A.alloc(4 * 512, BF16).rearrange("p (c t) -> p c t", t=512); cqnb = Buf("cqn")
                ckvn = A.alloc(2 * 512, BF16).rearrange("p (c t) -> p c t", t=512); ckvnb = Buf("ckvn")
                cos_ = A.alloc(512, F32); sin_ = A.alloc(512, F32); csb = Buf("cossin")
                ang = A.alloc(512, F32); angb = Buf("ang")
                ki32 = A.alloc(512, I32); kib = Buf("ki32")
                Qn = [A.alloc(512, BF16) for _ in range(nh)]; Qr = [A.alloc(512, BF16) for _ in range(nh)]
                qnb = [Buf("Qn%d" % i) for i in range(nh)]; qrb = [Buf("Qr%d" % i) for i in range(nh)]
                tring = mk_tiles(4, "mt")
                rs = A.alloc(512, F32); rsb = Buf("mrs")
                r32 = A.alloc(512, F32); r32b = Buf("r32")
                t1 = A.alloc(512, F32); t1b = Buf("t1"); t2 = A.alloc(512, F32); t2b = Buf("t2")
                pft = [A.alloc(512, F32) for _ in range(2)]
                pfr = Ring([(pft[i], Buf("mpf%d" % i)) for i in range(2)])
                pmt = [A.alloc(512, BF16) for _ in range(3)]
                pmr = Ring([(pmt[i], Buf("mpm%d" % i)) for i in range(3)])
                ost = [A.alloc(512, F32) for _ in range(2)]
                osr = Ring([(ost[i], Buf("mos%d" % i)) for i in range(2)])
                rec = A.alloc(512, F32); recb = Buf("rec")
                psS = Ring([(psum[i], psb[i]) for i in range(3)])
                psP = Ring([(psum[3], psb[3]), (psum[4], psb[4])])
                pso = (psum[5], psb[5]); pssum = (psum[6], psb[6]); pst = (psum[7], psb[7])
                scale = 192.0 ** -0.5

                def rope(src, srcb, dst_ap, dstb):
                    ps, pb_ = psP.next()
                    S.op("pe", lambda e, ps=ps: e.matmul(ps[:64, :], rotT[:64, :64], src[:64, :], start=True, stop=True),
                         reads=[srcb, cstb], writes=[pb_])
                    S.op("dve", lambda e: e.tensor_tensor(out=t1[:64, :], in0=src[:64, :], in1=cos_[:64, :], op=ALU.mult),
                         reads=[srcb, csb], writes=[t1b])
                    S.op("dve", lambda e, ps=ps: e.tensor_tensor(out=t2[:64, :], in0=ps[:64, :], in1=sin_[:64, :], op=ALU.mult),
                         reads=[pb_, csb], writes=[t2b])
                    S.op("dve", lambda e: e.tensor_tensor(out=dst_ap, in0=t1[:64, :], in1=t2[:64, :], op=ALU.add),
                         reads=[t1b, t2b], writes=[dstb])

                for ti in range(NTILE):
                    t0 = ti * TT
                    S.dma("sp", cq32, Zs[1536:2048, t0:t0 + TT].rearrange("(c p) t -> p c t", p=128), reads=[zb], writes=[cqb])
                    S.dma("sp", ckv32, Zs[2048:2304, t0:t0 + TT].rearrange("(c p) t -> p c t", p=128), reads=[zb], writes=[ckvb])
                    S.dma("sp", kr32[:64, :], Zs[2304:2368, t0:t0 + TT], reads=[zb], writes=[kr32b])
                    load_pos(t0)
                    C1 = 6.28125
                    C2 = 2 * PI - 6.28125
                    BND = 3.141592
                    S.op("dve", lambda e: e.tensor_scalar(out=ang[:64, :], in0=posF[:64, :], scalar1=aprm[:64, 12:13],
                                                          scalar2=None, op0=ALU.mult), reads=[posb, apb], writes=[angb])
                    S.op("dve", lambda e: e.tensor_scalar(out=t1[:64, :], in0=ang[:64, :], scalar1=1.0 / (2 * PI), scalar2=None,
                                                          op0=ALU.mult), reads=[angb], writes=[t1b])
                    S.op("dve", lambda e: e.tensor_copy(out=ki32[:64, :], in_=t1[:64, :]), reads=[t1b], writes=[kib])
                    S.op("dve", lambda e: e.tensor_copy(out=t1[:64, :], in_=ki32[:64, :]), reads=[kib], writes=[t1b])
                    S.op("dve", lambda e: e.scalar_tensor_tensor(out=t2[:64, :], in0=t1[:64, :], scalar=-C1, in1=ang[:64, :],
                                                                 op0=ALU.mult, op1=ALU.add), reads=[t1b, angb], writes=[t2b])
                    S.op("dve", lambda e: e.scalar_tensor_tensor(out=ang[:64, :], in0=t1[:64, :], scalar=-C2, in1=t2[:64, :],
                                                                 op0=ALU.mult, op1=ALU.add), reads=[t1b, t2b], writes=[angb])
                    for which, dst in ((0, sin_), (1, cos_)):
                        if which == 1:
                            S.op("dve", lambda e: e.tensor_scalar(out=ang[:64, :], in0=ang[:64, :], scalar1=PI / 2, scalar2=None,
                                                                  op0=ALU.add), reads=[angb], writes=[angb])
                        S.op("dve", lambda e: e.tensor_scalar(out=t1[:64, :], in0=ang[:64, :], scalar1=PI, scalar2=-2 * PI,
                                                              op0=ALU.is_gt, op1=ALU.mult), reads=[angb], writes=[t1b])
                        S.op("dve", lambda e: e.tensor_tensor(out=t1[:64, :], in0=t1[:64, :], in1=ang[:64, :], op=ALU.add),
                             reads=[t1b, angb], writes=[t1b])
                        S.op("dve", lambda e: e.tensor_scalar(out=t1[:64, :], in0=t1[:64, :], scalar1=-BND, scalar2=BND,
                                                              op0=ALU.max, op1=ALU.min), reads=[t1b], writes=[t1b])
                        S.op("act", lambda e, dst=dst: e.activation(out=dst[:64, :], in_=t1[:64, :], func=AF.Sin),
                             reads=[t1b], writes=[csb])
                    stat_rstd([(cq32[:, c, :], 128, [cqb]) for c in range(4)], 512, rs, rsb, tring, pst)
                    for c in range(4):
                        S.op("dve", lambda e, c=c: e.scalar_tensor_tensor(
                            out=cqn[:, c, :], in0=cq32[:, c, :], scalar=aprm[:, 2 + c:3 + c], in1=rs, op0=ALU.mult, op1=ALU.mult),
                            reads=[cqb, rsb, apb], writes=[cqnb])
                    stat_rstd([(ckv32[:, c, :], 128, [ckvb]) for c in range(2)], 256, rs, rsb, tring, pst)
                    for c in range(2):
                        S.op("dve", lambda e, c=c: e.scalar_tensor_tensor(
                            out=ckvn[:, c, :], in0=ckv32[:, c, :], scalar=aprm[:, 6 + c:7 + c], in1=rs, op0=ALU.mult, op1=ALU.mult),
                            reads=[ckvb, rsb, apb], writes=[ckvnb])
                    for hh in range(nh):
                        hd = h0 + hh
                        pq, pqb = psP.next()
                        for c in range(4):
                            S.op("pe", lambda e, pq=pq, c=c, hd=hd: e.matmul(pq[:, :], wuq[:, c, hd * 192:hd * 192 + 128], cqn[:, c, :],
                                                                             start=(c == 0), stop=(c == 3)), reads=[wb, cqnb], writes=[pqb])
                        pr, prb = psP.next()
                        for c in range(4):
                            S.op("pe", lambda e, pr=pr, c=c, hd=hd: e.matmul(pr[:64, :], wuq[:, c, hd * 192 + 128:hd * 192 + 192], cqn[:, c, :],
                                                                             start=(c == 0), stop=(c == 3)), reads=[wb, cqnb], writes=[prb])
                        stat_rstd([(pq[:, :], 128, [pqb]), (pr[:64, :], 64, [prb])], 192, rs, rsb, tring, pst)
                        S.op("dve", lambda e, pq=pq, hh=hh: e.scalar_tensor_tensor(
                            out=Qn[hh], in0=pq[:, :], scalar=aprm[:, 8:9], in1=rs, op0=ALU.mult, op1=ALU.mult),
                            reads=[pqb, rsb, apb], writes=[qnb[hh]])
                        S.op("dve", lambda e, pr=pr: e.scalar_tensor_tensor(
                            out=r32[:64, :], in0=pr[:64, :], scalar=aprm[:64, 9:10], in1=rs[:64, :], op0=ALU.mult, op1=ALU.mult),
                            reads=[prb, rsb, apb], writes=[r32b])
                        rope(r32, r32b, Qr[hh][:64, :], qrb[hh])
                        pk, pkb = psP.next()
                        for c in range(2):
                            S.op("pe", lambda e, pk=pk, c=c, hd=hd: e.matmul(pk[:, :], wukv[:, c, hd * 256:hd * 256 + 128], ckvn[:, c, :],
                                                                             start=(c == 0), stop=(c == 1)), reads=[wb, ckvnb], writes=[pkb])
                        stat_rstd([(pk[:, :], 128, [pkb]), (kr32[:64, :], 64, [kr32b])], 192, rs, rsb, tring, pst)
                        S.op("dve", lambda e, pk=pk, hh=hh, t0=t0: e.scalar_tensor_tensor(
                            out=Kn[hh][:, t0:t0 + TT], in0=pk[:, :], scalar=aprm[:, 10:11], in1=rs, op0=ALU.mult, op1=ALU.mult),
                            reads=[pkb, rsb, apb], writes=[knb[hh]])
                        S.op("dve", lambda e: e.scalar_tensor_tensor(
                            out=r32[:64, :], in0=kr32[:64, :], scalar=aprm[:64, 11:12], in1=rs[:64, :], op0=ALU.mult, op1=ALU.mult),
                            reads=[kr32b, rsb, apb], writes=[r32b])
                        rope(r32, r32b, Kr[hh][:64, t0:t0 + TT], krb[hh])
                        for tb_ in range(4):
                            pv, pvb = psP.next()
                            for c in range(2):
                                S.op("pe", lambda e, pv=pv, c=c, hd=hd, tb_=tb_: e.matmul(
                                    pv[:, :128], ckvn[:, c, tb_ * 128:(tb_ + 1) * 128], wukv[:, c, hd * 256 + 128:hd * 256 + 256],
                                    start=(c == 0), stop=(c == 1)), reads=[wb, ckvnb], writes=[pvb])
                            S.op("act", lambda e, pv=pv, hh=hh, tb_=tb_, ti=ti: e.activation(
                                out=Vp[hh][:, 4 * ti + tb_, :], in_=pv[:, :128], func=AF.Copy), reads=[pvb], writes=[vpb[hh]])
                    for hh in range(nh):
                        hd = h0 + hh
                        nkb = 4 * (ti + 1)
                        for kb in range(nkb):
                            ps, pb_ = psS.next()
                            S.op("pe", lambda e, ps=ps, kb=kb, hh=hh: e.matmul(ps[:, :], Kn[hh][:, kb * 128:(kb + 1) * 128], Qn[hh],
                                                                                start=True, stop=False), reads=[knb[hh], qnb[hh]], writes=[pb_])
                            S.op("pe", lambda e, ps=ps, kb=kb, hh=hh: e.matmul(ps[:, :], Kr[hh][:64, kb * 128:(kb + 1) * 128], Qr[hh][:64, :],
                                                                                start=False, stop=True), reads=[krb[hh], qrb[hh]], writes=[pb_])
                            pm, pmb = pmr.next()
                            if kb >= 4 * ti:
                                pf, pfb = pfr.next()
                                S.op("act", lambda e, pf=pf, ps=ps: e.activation(out=pf, in_=ps[:, :], func=AF.Exp, scale=scale),
                                     reads=[pb_], writes=[pfb])
                                S.op("dve", lambda e, pm=pm, pf=pf, kb=kb, ti=ti: e.tensor_tensor(
                                    out=pm, in0=pf, in1=cst[:, 2 + kb - 4 * ti, :], op=ALU.mult), reads=[pfb, cstb], writes=[pmb])
                            else:
                                S.op("act", lambda e, pm=pm, ps=ps: e.activation(out=pm, in_=ps[:, :], func=AF.Exp, scale=scale),
                                     reads=[pb_], writes=[pmb])
                            S.op("pe", lambda e, pm=pm, kb=kb, hh=hh, nkb=nkb: e.matmul(pso[0][:, :], Vp[hh][:, kb, :], pm,
                                                                                        start=(kb == 0), stop=(kb == nkb - 1)),
                                 reads=[vpb[hh], pmb], writes=[pso[1]])
                            S.op("pe", lambda e, pm=pm, kb=kb, nkb=nkb: e.matmul(pssum[0][:, :], onesb, pm,
                                                                                 start=(kb == 0), stop=(kb == nkb - 1)),
                                 reads=[cb_, pmb], writes=[pssum[1]])
                        S.op("dve", lambda e: e.reciprocal(out=rec, in_=pssum[0][:, :]), reads=[pssum[1]], writes=[recb])
                        o_, o_b = osr.next()
                        S.op("dve", lambda e, o_=o_: e.tensor_tensor(out=o_, in0=pso[0][:, :], in1=rec, op=ALU.mult),
                             reads=[pso[1], recb], writes=[o_b])
                        S.dma("sp", Os[1024 + hd * 128:1024 + (hd + 1) * 128, t0:t0 + TT], o_, reads=[o_b], writes=[osb], cb=o_b)

            for grp in range(4):
                swa(grp)
                S.barrier()
            if l + 1 < depth:
                precast(l + 1)
            for h0 in range(8):
                mla(h0, 1)
                S.barrier()

        precast(0)
        for l in range(depth):
            token_phase("first" if l == 0 else "mid", l)
            S.barrier()
            attn_phase(l)
        token_phase("last", depth)
        S.emit(final_bufs=[outb])
    return nc


def _lay(g):
    return np.ascontiguousarray(np.asarray(g, np.float32).reshape(-1, 128).T)


def _host_constants():
    k = np.arange(128)[:, None]; q = np.arange(128)[None, :]
    c = np.zeros((8, 128, 512), np.float32)
    c[0] = np.tile((k > q).astype(np.float32), (1, 4))
    c[1] = np.tile((k <= q).astype(np.float32), (1, 4))
    qq = np.arange(512)[None, :]
    for kb in range(4):
        c[2 + kb] = ((kb * 128 + k) <= qq).astype(np.float32)
    R = np.zeros((64, 64), np.float32)
    for m in range(32):
        R[m + 32, m] = -1.0
        R[m, m + 32] = 1.0
    c[6, :64, :64] = R
    return c


def kernel(x, p, positions, ffn1_norm, ffn1_w_gate, ffn1_w_up, ffn1_w_down, mix_norm, w_in, swa_q_norm,
           swa_k_norm, swa_sinks, mla_q_lora_norm, mla_w_uq, mla_kv_lora_norm, mla_w_ukv, mla_q_norm,
           mla_k_norm, out_norm_swa, out_norm_mla, w_o, ffn2_norm, ffn2_w_gate, ffn2_w_up, ffn2_w_down,
           ple_proj, ple_proj_norm, ple_gate_norm, ple_gate):
    x = np.asarray(x); p = np.asarray(p); positions = np.asarray(positions)
    B, T, _ = x.shape
    depth = p.shape[0]
    f = lambda a: np.ascontiguousarray(np.asarray(a, np.float32))
    gains = np.stack([np.concatenate([
        _lay(ffn1_norm[l]), _lay(mix_norm[l]),
        _lay(np.concatenate([np.asarray(out_norm_swa[l]), np.asarray(out_norm_mla[l])])),
        _lay(ffn2_norm[l]), _lay(ple_gate_norm[l]), _lay(ple_proj_norm[l])], axis=1) for l in range(depth)])
    half = 32
    inv_freq = (10000.0 ** (-np.arange(half, dtype=np.float32) / half)).astype(np.float32)
    slopes = (2.0 ** (-8.0 * np.arange(1, 17, dtype=np.float32) / 16)).astype(np.float32)
    aprm = np.zeros((depth, 128, 32), np.float32)
    for l in range(depth):
        a = aprm[l]
        a[:64, 0] = np.asarray(swa_q_norm[l]); a[:64, 1] = np.asarray(swa_k_norm[l])
        a[:, 2:6] = _lay(mla_q_lora_norm[l]); a[:, 6:8] = _lay(mla_kv_lora_norm[l])
        qn = np.asarray(mla_q_norm[l]); kn = np.asarray(mla_k_norm[l])
        a[:, 8] = qn[:128]; a[:64, 9] = qn[128:]; a[:, 10] = kn[:128]; a[:64, 11] = kn[128:]
        a[:32, 12] = inv_freq; a[32:64, 12] = inv_freq
        a[:, 13:29] = (-8.0 * slopes)[None, :]
        a[:, 30] = np.float32(PI)
    consts = _host_constants()
    sinkrow = np.zeros((depth, 4, 64, 512), np.float32)
    for l in range(depth):
        s = np.asarray(swa_sinks[l], np.float32)
        for grp in range(4):
            sinkrow[l, grp] = np.repeat(s[grp * 4:(grp + 1) * 4], 128)[None, :]
    f32 = lambda a: np.ascontiguousarray(np.asarray(a, np.float32))
    wnames = dict(f1g=ffn1_w_gate, f1u=ffn1_w_up, f1d=ffn1_w_down, w_in=w_in, wuq=mla_w_uq, wukv=mla_w_ukv,
                  w_o=w_o, f2g=ffn2_w_gate, f2u=ffn2_w_up, f2d=ffn2_w_down, pproj=ple_proj, pgate=ple_gate)
    wnames = {k: np.asarray(v) for k, v in wnames.items()}
    nc = build_program(T, 1)
    cur = [np.ascontiguousarray(x[b].T) for b in range(B)]
    posrow = [np.ascontiguousarray(np.broadcast_to(positions[b][None, :], (128, T))).astype(np.int32) for b in range(B)]
    poscol = [np.ascontiguousarray(positions[b].reshape(T // 128, 128).T).astype(np.int32) for b in range(B)]
    for l in range(depth):
        shared = {k: f32(v[l:l + 1]) for k, v in wnames.items()}
        shared.update(gains=f32(gains[l:l + 1]), consts=f32(consts), sinkrow=f32(sinkrow[l:l + 1]), aprm=f32(aprm[l:l + 1]))
        in_maps = []
        for b in range(B):
            m = dict(shared)
            m["xT"] = cur[b]
            m["pT"] = np.ascontiguousarray(np.transpose(p[l:l + 1, b], (0, 2, 1)))
            m["posrow"] = posrow[b]
            m["poscol"] = poscol[b]
            in_maps.append(m)
        res = run_bass_kernel_spmd(nc, in_maps, core_ids=list(range(B)))
        cur = [np.ascontiguousarray(res.results[b]["yT"]) for b in range(B)]
    out = np.stack([np.ascontiguousarray(cur[b].T) for b in range(B)])
    return out.astype(np.float32)
```

```python
import math
from contextlib import ExitStack

import numpy as np
import concourse.bass as bass
import concourse.mybir as mybir
from concourse.bass_utils import run_bass_kernel_spmd

F32 = mybir.dt.float32
BF16 = mybir.dt.bfloat16
I32 = mybir.dt.int32
AF = mybir.ActivationFunctionType
ALU = mybir.AluOpType

ENGS = ("pe", "act", "dve", "pool", "sp")

D = 2048; DFF = 5632; INC = 2368; PLE = 256
DC = D // 128; FC = DFF // 128
EPS = 1e-6
DEPTH = 4; BATCH = 2; SEQ = 8192
TT = 512
PI = math.pi


class Buf:
    __slots__ = ("name", "w", "r", "chan")

    def __init__(self, name):
        self.name = name
        self.w = {}
        self.r = {}
        self.chan = None


class Sched:
    def __init__(self, nc):
        self.nc = nc
        self.ops = {e: [] for e in ENGS}
        self.nchan = 0
        self.chan_cnt = []
        self.miles = {e: set() for e in ENGS}
        self.pending_barrier = {e: None for e in ENGS}
        self.chan_by_name = {}

    @staticmethod
    def _merge(d, key, val):
        if d.get(key, -1) < val:
            d[key] = val

    def _collect(self, eng, reads, writes):
        deps = {}
        for b in reads:
            for k, v in b.w.items():
                self._merge(deps, k, v)
        for b in writes:
            for k, v in b.w.items():
                self._merge(deps, k, v)
            for k, v in b.r.items():
                self._merge(deps, k, v)
        own = ("e", eng)
        if own in deps:
            raw = -1
            if eng != "pe":
                for b in reads:
                    raw = max(raw, b.w.get(own, -1))
            if raw >= 0:
                deps[own] = raw
            else:
                del deps[own]
        return deps

    def _commit(self, token, reads, writes):
        k, v = token
        for b in reads:
            self._merge(b.r, k, v)
        for b in writes:
            if b.r:
                b.w = {}
                b.r = {}
            self._merge(b.w, k, v)

    def _take_barrier(self, eng, deps, keep_own=False):
        bd = self.pending_barrier[eng]
        if bd is not None:
            for k, v in bd.items():
                if k == ("e", eng) and not keep_own:
                    continue
                self._merge(deps, k, v)
            self.pending_barrier[eng] = None

    def op(self, eng, fn, reads=(), writes=()):
        deps = self._collect(eng, reads, writes)
        self._take_barrier(eng, deps)
        seq = len(self.ops[eng])
        self.ops[eng].append([fn, deps, None])
        self._commit((("e", eng), seq), reads, writes)

    def new_chan(self):
        self.chan_cnt.append(0)
        self.nchan += 1
        return self.nchan - 1

    def dma(self, eng, out, in_, reads=(), writes=(), cb=None):
        b = cb if cb is not None else (list(writes) + list(reads))[0]
        if b.chan is None:
            if b.name not in self.chan_by_name:
                self.chan_by_name[b.name] = self.new_chan()
            b.chan = self.chan_by_name[b.name]
        chan = b.chan
        deps = self._collect("dma", reads, writes)
        self._take_barrier(eng, deps, keep_own=True)
        self.chan_cnt[chan] += 16
        val = self.chan_cnt[chan]

        def fn(e, out=out, in_=in_):
            return e.dma_start(out=out, in_=in_)
        self.ops[eng].append([fn, deps, (chan, val)])
        self._commit((("d", chan), val), reads, writes)

    def barrier(self):
        bd = {}
        for e in ENGS:
            if self.ops[e]:
                for i in range(len(self.ops[e]) - 1, -1, -1):
                    if self.ops[e][i][2] is None:
                        bd[("e", e)] = i
                        break
        for c in range(self.nchan):
            if self.chan_cnt[c]:
                bd[("d", c)] = self.chan_cnt[c]
        for e in ENGS:
            old = self.pending_barrier[e]
            if old is not None:
                for k, v in old.items():
                    self._merge(bd, k, v)
            self.pending_barrier[e] = dict(bd)

    def emit(self, final_bufs=()):
        nc = self.nc
        fdeps = {}
        for b in final_bufs:
            for k, v in b.w.items():
                self._merge(fdeps, k, v)
        for e in ENGS:
            for fn, deps, tok in self.ops[e]:
                for (kind, key), v in deps.items():
                    if kind == "e":
                        self.miles[key].add(v)
        for (kind, key), v in fdeps.items():
            if kind == "e":
                self.miles[key].add(v)
        mile_idx = {}
        for e in ENGS:
            for i, s in enumerate(sorted(self.miles[e])):
                mile_idx[(e, s)] = i + 1
        with ExitStack() as st:
            esem = {e: st.enter_context(nc.semaphore("sem_" + e)) for e in ENGS}
            csem = [st.enter_context(nc.semaphore("dsem%d" % i)) for i in range(self.nchan)]
            block = st.enter_context(nc.Block())

            def run(e, engobj):
                known = {}
                for seq, (fn, deps, tok) in enumerate(self.ops[e]):
                    waits = []
                    for (kind, key), v in deps.items():
                        if kind == "e":
                            need = mile_idx[(key, v)]
                            sem = esem[key]
                        else:
                            need = v
                            sem = csem[key]
                        if known.get((kind, key), 0) < need:
                            waits.append((sem, need))
                            known[(kind, key)] = need
                    for sem, need in waits[:-1]:
                        engobj.wait_ge(sem, need)
                    ins = fn(engobj)
                    if waits:
                        ins._wait_ge(waits[-1][0], waits[-1][1])
                    if tok is not None:
                        ins.then_inc(csem[tok[0]], 16)
                    elif (e, seq) in mile_idx:
                        ins.then_inc(esem[e], 1)
                if e == "sp":
                    for (kind, key), v in fdeps.items():
                        if kind == "e":
                            engobj.wait_ge(esem[key], mile_idx[(key, v)])
                        else:
                            engobj.wait_ge(csem[key], v)

            block.tensor(lambda eng: run("pe", eng))
            block.scalar(lambda eng: run("act", eng))
            block.vector(lambda eng: run("dve", eng))
            block.gpsimd(lambda eng: run("pool", eng))
            block.sync(lambda eng: run("sp", eng))


class Ring:
    def __init__(self, items):
        self.items = items
        self.i = 0

    def next(self):
        it = self.items[self.i % len(self.items)]
        self.i += 1
        return it


class Arena:
    def __init__(self, raw, nwords):
        self.raw = raw
        self.n = nwords
        self.off = 0

    def reset(self):
        self.off = 0

    def alloc(self, free_elems, dt):
        words = free_elems if dt != BF16 else (free_elems + 1) // 2
        words = (words + 7) // 8 * 8
        assert self.off + words <= self.n, ("SBUF arena overflow", self.off, words, self.n)
        v = self.raw[:, self.off:self.off + words]
        self.off += words
        if dt == BF16:
            v = v.bitcast(BF16)[:, :free_elems]
        elif dt == I32:
            v = v.bitcast(I32)[:, :free_elems]
        else:
            v = v[:, :free_elems]
        return v


def build_program(T, depth):
    NTILE = T // TT
    NB = T // 128
    nc = bass.Bass("TRN2", target_bir_lowering=False)
    din = lambda n, s, dt=F32: nc.dram_tensor(n, s, dt, kind="ExternalInput").ap()
    xT = din("xT", [D, T])
    pT = din("pT", [depth, PLE, T])
    posrow = din("posrow", [128, T], I32)
    poscol = din("poscol", [128, NB], I32)
    gains_d = din("gains", [depth, 128, 96])
    aprm_d = din("aprm", [depth, 128, 32])
    sinkrow_d = din("sinkrow", [depth, 4, 64, 512])
    consts_d = din("consts", [8, 128, 512])
    Wn = {}
    for nm, shp in (("f1g", [D, DFF]), ("f1u", [D, DFF]), ("f1d", [DFF, D]), ("w_in", [D, INC]),
                    ("wuq", [512, 1536]), ("wukv", [256, 2048]), ("w_o", [D, D]),
                    ("f2g", [D, DFF]), ("f2u", [D, DFF]), ("f2d", [DFF, D]),
                    ("pproj", [PLE, D]), ("pgate", [D, D])):
        Wn[nm] = din(nm, [depth] + shp)
    yT = nc.dram_tensor("yT", [D, T], F32, kind="ExternalOutput").ap()
    Hs = nc.dram_tensor("Hs", [D, T], F32).ap()
    Zs = nc.dram_tensor("Zs", [INC, T], F32).ap()
    VAs = nc.dram_tensor("VAs", [T, 256], F32).ap()
    Os = nc.dram_tensor("Os", [D, T], F32).ap()

    def slabs(K, N, ns):
        return (N + ns - 1) // ns, K // 128, ns
    wspec = {"f1g": (D, DFF, 512), "f1u": (D, DFF, 512), "f1d": (DFF, D, 128), "w_in": (D, INC, 512),
             "w_o": (D, D, 512), "f2g": (D, DFF, 512), "f2u": (D, DFF, 512), "f2d": (DFF, D, 128),
             "pproj": (PLE, D, 512), "pgate": (D, D, 512), "w_inv": (D, 256, 256)}
    wscr = {}
    for par in range(2):
        for nm, (K, N, ns) in wspec.items():
            nsl, kc, _ = slabs(K, N, ns)
            wscr[(nm, par)] = nc.dram_tensor("ws_%s_%d" % (nm, par), [nsl, 128, kc * ns], BF16).ap()

    NWORDS = 46 * 1024
    with ExitStack() as st:
        raw = st.enter_context(nc.sbuf_tensor("raw", [128, NWORDS], F32))
        psum = [st.enter_context(nc.psum_tensor("ps%d" % i, [128, 512], F32)) for i in range(8)]
        A = Arena(raw, NWORDS)
        S = Sched(nc)
        psb = [Buf("ps%d" % i) for i in range(8)]
        hsb = [Buf("Hs%d" % t) for t in range(NTILE)]
        zb = Buf("Zs"); vab = Buf("VAs"); osb = Buf("Os"); outb = Buf("out")
        wsb = {k: Buf("ws_%s_%d" % k) for k in wscr}

        ones32 = A.alloc(128, F32); onesb = A.alloc(128, BF16)
        gains = A.alloc(96, F32); aprm = A.alloc(32, F32)
        base_off = None
        cb_ = Buf("consts"); gb = Buf("gains"); apb = Buf("aprm")
        S.op("dve", lambda e: e.memset(ones32, 1.0), writes=[cb_])
        S.op("dve", lambda e: e.memset(onesb, 1.0), writes=[cb_])
        base_off = A.off

        def precast(l):
            par = l % 2
            for nm, (K, N, ns) in wspec.items():
                src = Wn["w_in"][l][:, 1280:1536] if nm == "w_inv" else Wn[nm][l]
                sv = src.rearrange("(k p) n -> p k n", p=128)
                nsl, kc, _ = slabs(K, N, ns)
                for s in range(nsl):
                    w = min(ns, N - s * ns)
                    dst = wscr[(nm, par)][s][:, :kc * w].rearrange("p (k n) -> p k n", n=w)
                    S.dma("pool", dst, sv[:, :, s * ns:s * ns + w], writes=[wsb[(nm, par)]], cb=wsb[(nm, par)])

        def token_phase(mode, l):
            front = mode in ("mid", "last")
            back = mode in ("first", "mid")
            lf = l - 1 if front else None
            A.off = base_off
            WSLOT = 8192
            h = A.alloc(DC * TT, F32).rearrange("p (c t) -> p c t", t=TT)
            nT = A.alloc(DC * TT, BF16).rearrange("p (c t) -> p c t", t=TT)
            big = A.alloc(FC * TT, BF16)
            wslots = [A.alloc(WSLOT, BF16) for _ in range(4)]
            tmp = [A.alloc(TT, F32) for _ in range(4)]
            stg = [A.alloc(TT, F32) for _ in range(3)]
            rstd = [A.alloc(TT, F32) for _ in range(2)]
            pTs = A.alloc(2 * TT, BF16).rearrange("p (c t) -> p c t", t=TT)
            hb = [Buf("h%d" % c) for c in range(DC)]
            nb = [Buf("n%d" % c) for c in range(DC)]
            bigb = [Buf("big%d" % c) for c in range(FC)]
            wring = Ring([(wslots[i], Buf("w%d" % i)) for i in range(4)])
            psring = Ring([(psum[i], psb[i]) for i in range(6)])
            ps_stat = [(psum[6], psb[6]), (psum[7], psb[7])]
            tmpring = Ring([(tmp[i], Buf("tmp%d" % i)) for i in range(4)])
            stgring = Ring([(stg[i], Buf("stg%d" % i)) for i in range(3)])
            rstdb = [Buf("rstd0"), Buf("rstd1")]
            pb = Buf("pTs")
            actT = big.rearrange("p (f t) -> p f t", t=TT)
            big32 = big.bitcast(F32).rearrange("p (f t) -> p f t", t=TT)

            def b32(c):
                return [bigb[2 * c], bigb[2 * c + 1]]

            def linear(nm, lw, K, N, rhs_fn, rhs_bufs, evac):
                _, _, ns = wspec[nm]
                kc = K // 128
                scr = wscr[(nm, lw % 2)]; scb = wsb[(nm, lw % 2)]
                for si, n0 in enumerate(range(0, N, ns)):
                    w = min(ns, N - n0)
                    slot, sbuf_ = wring.next()
                    S.dma("sp", slot[:, :kc * w], scr[si][:, :kc * w], reads=[scb], writes=[sbuf_])
                    sv = slot[:, :kc * w].rearrange("p (k n) -> p k n", n=w)
                    for m0 in range(0, w, 128):
                        mw = min(128, w - m0)
                        ps, pb_ = psring.next()
                        for k in range(kc):
                            S.op("pe", lambda e, ps=ps, sv=sv, k=k, m0=m0, mw=mw: e.matmul(
                                ps[:mw, :TT], sv[:, k, m0:m0 + mw], rhs_fn(k), start=(k == 0), stop=(k == kc - 1)),
                                reads=[sbuf_, rhs_bufs[k]], writes=[pb_])
                        evac((n0 + m0) // 128, ps, pb_, mw)

            def rms_stats(src_fn, src_bufs, chunks, n_feat, which):
                ps, pb_ = ps_stat[which]
                for i, c in enumerate(chunks):
                    t, tb = tmpring.next()
                    S.op("act", lambda e, t=t, c=c: e.activation(out=t, in_=src_fn(c), func=AF.Square),
                         reads=src_bufs(c), writes=[tb])
                    S.op("pe", lambda e, t=t, i=i: e.matmul(ps[:, :TT], ones32, t, start=(i == 0),
                                                            stop=(i == len(chunks) - 1)),
                         reads=[tb, cb_], writes=[pb_])
                t, tb = tmpring.next()
                S.op("act", lambda e, t=t: e.activation(out=t, in_=ps[:, :TT], func=AF.Sqrt,
                                                        scale=1.0 / n_feat, bias=EPS), reads=[pb_], writes=[tb])
                S.op("dve", lambda e, t=t: e.reciprocal(out=rstd[which], in_=t), reads=[tb], writes=[rstdb[which]])

            def rmsnorm_h(gcol):
                rms_stats(lambda c: h[:, c, :], lambda c: [hb[c]], list(range(DC)), D, 0)
                for c in range(DC):
                    S.op("dve", lambda e, c=c: e.scalar_tensor_tensor(
                        out=nT[:, c, :], in0=h[:, c, :], scalar=gains[:, gcol + c:gcol + c + 1], in1=rstd[0],
                        op0=ALU.mult, op1=ALU.mult), reads=[hb[c], rstdb[0], gb], writes=[nb[c]])

            def ffn(pre, lw, gcol):
                rmsnorm_h(gcol)
                sg_, su_ = wscr[(pre + "g", lw % 2)], wscr[(pre + "u", lw % 2)]
                sgb_, sub_ = wsb[(pre + "g", lw % 2)], wsb[(pre + "u", lw % 2)]
                for si, n0 in enumerate(range(0, DFF, 512)):
                    sg, sgb = wring.next()
                    su, sub = wring.next()
                    S.dma("sp", sg[:, :DC * 512], sg_[si], reads=[sgb_], writes=[sgb])
                    S.dma("sp", su[:, :DC * 512], su_[si], reads=[sub_], writes=[sub])
                    sgv = sg[:, :DC * 512].rearrange("p (k n) -> p k n", n=512)
                    suv = su[:, :DC * 512].rearrange("p (k n) -> p k n", n=512)
                    for m0 in range(0, 512, 128):
                        f = (n0 + m0) // 128
                        pg, pgb = psring.next()
                        pu, pub = psring.next()
                        for (pp, ppb, vv, vb) in ((pg, pgb, sgv, sgb), (pu, pub, suv, sub)):
                            for k in range(DC):
                                S.op("pe", lambda e, pp=pp, vv=vv, k=k, m0=m0: e.matmul(
                                    pp[:, :TT], vv[:, k, m0:m0 + 128], nT[:, k, :], start=(k == 0), stop=(k == DC - 1)),
                                    reads=[vb, nb[k]], writes=[ppb])
                        t, tb = tmpring.next()
                        S.op("act", lambda e, t=t, pg=pg: e.activation(out=t, in_=pg[:, :TT], func=AF.Silu),
                             reads=[pgb], writes=[tb])
                        S.op("dve", lambda e, t=t, pu=pu, f=f: e.tensor_tensor(
                            out=actT[:, f, :], in0=t, in1=pu[:, :TT], op=ALU.mult),
                            reads=[tb, pub], writes=[bigb[f]])

                def evac_down(m, ps, pb_, mw):
                    S.op("dve", lambda e, m=m, ps=ps: e.scalar_tensor_tensor(
                        out=h[:, m, :], in0=ps[:, :TT], scalar=0.5, in1=h[:, m, :], op0=ALU.mult, op1=ALU.add),
                        reads=[pb_, hb[m]], writes=[hb[m]])
                linear(pre + "d", lw, DFF, D, lambda k: actT[:, k, :], bigb, evac_down)

            if back:
                S.dma("pool", gains[:, 0:32], gains_d[l][:, 0:32], writes=[gb])
            if front:
                S.dma("pool", gains[:, 32:96], gains_d[lf][:, 32:96], writes=[gb])
            for ti in range(NTILE):
                t0 = ti * TT
                if front:
                    S.dma("sp", h, Hs.rearrange("(c p) t -> p c t", p=128)[:, :, t0:t0 + TT], reads=[hsb[ti]], writes=hb)
                    S.dma("sp", big32[:, :DC, :], Os.rearrange("(c p) t -> p c t", p=128)[:, :, t0:t0 + TT],
                          reads=[osb], writes=bigb[:2 * DC])
                    rms_stats(lambda c: big32[:, c, :], b32, list(range(0, 8)), 1024, 0)
                    rms_stats(lambda c: big32[:, c, :], b32, list(range(8, 16)), 1024, 1)
                    for c in range(DC):
                        S.op("dve", lambda e, c=c: e.scalar_tensor_tensor(
                            out=nT[:, c, :], in0=big32[:, c, :], scalar=gains[:, 32 + c:33 + c], in1=rstd[c // 8],
                            op0=ALU.mult, op1=ALU.mult), reads=b32(c) + [rstdb[c // 8], gb], writes=[nb[c]])

                    def evac_wo(m, ps, pb_, mw):
                        S.op("dve", lambda e, m=m, ps=ps: e.tensor_tensor(
                            out=h[:, m, :], in0=ps[:, :TT], in1=h[:, m, :], op=ALU.add),
                            reads=[pb_, hb[m]], writes=[hb[m]])
                    linear("w_o", lf, D, D, lambda k: nT[:, k, :], nb, evac_wo)
                    ffn("f2", lf, 48)
                    S.dma("pool", pTs, pT[lf].rearrange("(c p) t -> p c t", p=128)[:, :, t0:t0 + TT], writes=[pb])
                    PP = big32

                    def evac_pp(m, ps, pb_, mw):
                        S.op("act", lambda e, m=m, ps=ps: e.activation(out=PP[:, m, :], in_=ps[:, :TT], func=AF.Copy),
                             reads=[pb_], writes=b32(m))
                    linear("pproj", lf, PLE, D, lambda k: pTs[:, k, :], [pb, pb], evac_pp)
                    rms_stats(lambda c: PP[:, c, :], b32, list(range(DC)), D, 1)
                    rmsnorm_h(64)

                    def evac_gate(m, ps, pb_, mw):
                        t, tb = tmpring.next()
                        S.op("act", lambda e, t=t, ps=ps: e.activation(out=t, in_=ps[:, :TT], func=AF.Sigmoid),
                             reads=[pb_], writes=[tb])
                        t2, t2b = tmpring.next()
                        S.op("dve", lambda e, t2=t2, m=m: e.scalar_tensor_tensor(
                            out=t2, in0=PP[:, m, :], scalar=gains[:, 80 + m:81 + m], in1=rstd[1],
                            op0=ALU.mult, op1=ALU.mult), reads=b32(m) + [rstdb[1], gb], writes=[t2b])
                        S.op("pool", lambda e, t=t, t2=t2: e.tensor_tensor(out=t2, in0=t2, in1=t, op=ALU.mult),
                             reads=[t2b, tb], writes=[t2b])
                        S.op("dve", lambda e, t2=t2, m=m: e.tensor_tensor(out=h[:, m, :], in0=h[:, m, :], in1=t2, op=ALU.add),
                             reads=[t2b, hb[m]], writes=[hb[m]])
                    linear("pgate", lf, D, D, lambda k: nT[:, k, :], nb, evac_gate)
                else:
                    S.dma("sp", h, xT.rearrange("(c p) t -> p c t", p=128)[:, :, t0:t0 + TT], writes=hb)
                if back:
                    ffn("f1", l, 0)
                    S.dma("sp", Hs.rearrange("(c p) t -> p c t", p=128)[:, :, t0:t0 + TT], h, reads=hb,
                          writes=[hsb[ti]], cb=hb[0])
                    rmsnorm_h(16)

                    def evac_z(m, ps, pb_, mw):
                        s_, s_b = stgring.next()
                        S.op("dve", lambda e, s_=s_, ps=ps, mw=mw: e.tensor_copy(out=s_[:mw, :], in_=ps[:mw, :TT]),
                             reads=[pb_], writes=[s_b])
                        S.dma("sp", Zs[m * 128:m * 128 + mw, t0:t0 + TT], s_[:mw, :], reads=[s_b], writes=[zb], cb=s_b)
                    linear("w_in", l, D, INC, lambda k: nT[:, k, :], nb, evac_z)
                    slot, sbuf_ = wring.next()
                    S.dma("sp", slot[:, :DC * 256], wscr[("w_inv", l % 2)][0], reads=[wsb[("w_inv", l % 2)]], writes=[sbuf_])
                    sv = slot[:, :DC * 256].rearrange("p (k n) -> p k n", n=256)
                    for tb_ in range(TT // 128):
                        ps, pb_ = psring.next()
                        for k in range(DC):
                            S.op("pe", lambda e, ps=ps, k=k, tb_=tb_, sv=sv: e.matmul(
                                ps[:, :256], nT[:, k, tb_ * 128:(tb_ + 1) * 128], sv[:, k, :], start=(k == 0),
                                stop=(k == DC - 1)), reads=[sbuf_, nb[k]], writes=[pb_])
                        s_, s_b = stgring.next()
                        S.op("dve", lambda e, s_=s_, ps=ps: e.tensor_copy(out=s_[:, :256], in_=ps[:, :256]),
                             reads=[pb_], writes=[s_b])
                        S.dma("sp", VAs[t0 + tb_ * 128:t0 + (tb_ + 1) * 128, :], s_[:, :256], reads=[s_b],
                              writes=[vab], cb=s_b)
                else:
                    S.dma("sp", yT.rearrange("(c p) t -> p c t", p=128)[:, :, t0:t0 + TT], h, reads=hb,
                          writes=[outb], cb=hb[0])

        def attn_phase(l):
            A.off = base_off
            par = l % 2
            cst = A.alloc(7 * 512, BF16).rearrange("p (m q) -> p m q", q=512)
            rotT = A.alloc(64, F32)
            posF = A.alloc(512, F32)
            posK = A.alloc(NB, F32)
            tmpi = A.alloc(512, I32)
            wuq = A.alloc(4 * 1536, BF16).rearrange("p (k n) -> p k n", n=1536)
            wukv = A.alloc(2 * 2048, BF16).rearrange("p (k n) -> p k n", n=2048)
            cstb = Buf("cst"); posb = Buf("pos"); wb = Buf("wattn"); tib = Buf("tmpi")
            S.dma("pool", aprm, aprm_d[l], writes=[apb])
            S.dma("pool", cst[:, 0:6, :], consts_d[0:6].rearrange("m p q -> p m q"), writes=[cstb])
            S.dma("sp", rotT[:64, :], consts_d[6][:64, :64], writes=[cstb])
            pkb_ = Buf("posK")
            S.dma("sp", tmpi[:, :NB], poscol[:, :], writes=[tib])
            S.op("dve", lambda e: e.tensor_copy(out=posK, in_=tmpi[:, :NB]), reads=[tib], writes=[pkb_])

            def load_pos(t0):
                S.dma("sp", tmpi, posrow[:, t0:t0 + TT], writes=[tib])
                S.op("dve", lambda e: e.tensor_copy(out=posF, in_=tmpi), reads=[tib], writes=[posb])
            S.dma("pool", wuq, Wn["wuq"][l].rearrange("(k p) n -> p k n", p=128), writes=[wb])
            S.dma("pool", wukv, Wn["wukv"][l].rearrange("(k p) n -> p k n", p=128), writes=[wb])
            mark = A.off

            def mk_tiles(n, nm):
                ts = [A.alloc(512, F32) for _ in range(n)]
                return Ring([(ts[i], Buf("%s%d" % (nm, i))) for i in range(n)])

            def stat_rstd(pieces, n_feat, rs, rsb, tring, pst):
                ps, pb_ = pst
                for i, (ap, rows, bufs) in enumerate(pieces):
                    t, tb = tring.next()
                    S.op("act", lambda e, t=t, ap=ap, rows=rows: e.activation(out=t[:rows, :], in_=ap, func=AF.Square),
                         reads=bufs, writes=[tb])
                    S.op("pe", lambda e, t=t, rows=rows, i=i, n=len(pieces): e.matmul(
                        ps[:, :], ones32[:rows, :], t[:rows, :], start=(i == 0), stop=(i == n - 1)),
                        reads=[tb, cb_], writes=[pb_])
                t, tb = tring.next()
                S.op("act", lambda e, t=t: e.activation(out=t, in_=ps[:, :], func=AF.Ln, scale=1.0 / n_feat, bias=EPS),
                     reads=[pb_], writes=[tb])
                S.op("act", lambda e, t=t: e.activation(out=rs, in_=t, func=AF.Exp, scale=-0.5), reads=[tb], writes=[rsb])

            def swa(grp):
                A.off = mark
                KAp = A.alloc(T, BF16)
                VAp = A.alloc(NB * 64, BF16).rearrange("p (b d) -> p b d", d=64)
                esink = A.alloc(512, F32)
                qa32 = A.alloc(4 * 512, F32).rearrange("p (g t) -> p g t", t=512)
                ka32 = A.alloc(512, F32)
                qab = A.alloc(4 * 512, BF16).rearrange("p (b g q) -> p b g q", g=4, q=128)
                tring = mk_tiles(4, "st")
                sbr = mk_tiles(2, "sb")
                pfr = mk_tiles(2, "pf")
                pmt = [A.alloc(512, BF16) for _ in range(4)]
                pmr = Ring([(pmt[i], Buf("pm%d" % i)) for i in range(4)])
                dst = [A.alloc(128, F32) for _ in range(2)]
                dsr = Ring([(dst[i], Buf("ds%d" % i)) for i in range(2)])
                rs = A.alloc(512, F32); rsb = Buf("rs")
                ost = [A.alloc(512, F32) for _ in range(2)]
                osr = Ring([(ost[i], Buf("os%d" % i)) for i in range(2)])
                den = A.alloc(512, F32); denb = Buf("den")
                kab = [Buf("KAp%d" % t) for t in range(NTILE)]; vapb = [Buf("VAp%d" % t) for t in range(NTILE)]
                esb = Buf("esink"); q32b = Buf("qa32"); k32b = Buf("ka32")
                qabb = Buf("qab")
                psS = Ring([(psum[i], psb[i]) for i in range(4)])
                pso = (psum[4], psb[4]); pssum = (psum[5], psb[5]); pst = (psum[6], psb[6])
                S.dma("sp", esink[:64, :], sinkrow_d[l][grp], writes=[esb])
                S.op("act", lambda e: e.activation(out=esink[:64, :], in_=esink[:64, :], func=AF.Exp), reads=[esb], writes=[esb])

                def prep(ti):
                    t0 = ti * TT
                    load_pos(t0)
                    S.dma("sp", qa32[:64], Zs[grp * 256:(grp + 1) * 256, t0:t0 + TT].rearrange("(g d) t -> d g t", d=64),
                          reads=[zb], writes=[q32b])
                    S.dma("sp", ka32[:64, :], Zs[1024 + grp * 64:1024 + (grp + 1) * 64, t0:t0 + TT], reads=[zb], writes=[k32b])
                    S.dma("pool", VAp[:, 4 * ti:4 * ti + 4, :],
                          VAs[t0:t0 + TT, grp * 64:(grp + 1) * 64].rearrange("(b p) d -> p b d", p=128),
                          reads=[vab], writes=[vapb[ti]])
                    for g in range(4):
                        stat_rstd([(qa32[:64, g, :], 64, [q32b])], 64, rs, rsb, tring, pst)
                        S.op("dve", lambda e, g=g: e.scalar_tensor_tensor(
                            out=qab[:64, :, g, :], in0=qa32[:64, g, :].rearrange("p (b q) -> p b q", q=128),
                            scalar=aprm[:64, 0:1], in1=rs[:64, :].rearrange("p (b q) -> p b q", q=128),
                            op0=ALU.mult, op1=ALU.mult), reads=[q32b, rsb, apb], writes=[qabb])
                    stat_rstd([(ka32[:64, :], 64, [k32b])], 64, rs, rsb, tring, pst)
                    S.op("dve", lambda e, t0=t0: e.scalar_tensor_tensor(
                        out=KAp[:64, t0:t0 + TT], in0=ka32[:64, :], scalar=aprm[:64, 1:2], in1=rs[:64, :],
                        op0=ALU.mult, op1=ALU.mult), reads=[k32b, rsb, apb], writes=[kab[ti]])

                def front(Bq):
                    qb = Bq % 4
                    kbs = [Bq - 1, Bq] if Bq > 0 else [Bq]
                    pms = []
                    for kbI in kbs:
                        ps, pb_ = psS.next()
                        S.op("pe", lambda e, ps=ps, kbI=kbI, qb=qb: e.matmul(
                            ps[:, :], KAp[:64, kbI * 128:(kbI + 1) * 128],
                            qab[:64, qb, :, :].rearrange("p g q -> p (g q)"), start=True, stop=True),
                            reads=[kab[kbI // 4], qabb], writes=[pb_])
                        d_, d_b = dsr.next()
                        S.op("dve", lambda e, d_=d_, kbI=kbI, qb=qb: e.tensor_scalar(
                            out=d_, in0=posF[:, qb * 128:(qb + 1) * 128], scalar1=posK[:, kbI:kbI + 1], scalar2=None,
                            op0=ALU.subtract), reads=[posb, pkb_], writes=[d_b])
                        S.op("dve", lambda e, d_=d_: e.scalar_tensor_tensor(
                            out=d_, in0=d_, scalar=-1.0, in1=d_, op0=ALU.mult, op1=ALU.max), reads=[d_b], writes=[d_b])
                        sb_, sb_b = sbr.next()
                        for g in range(4):
                            S.op("dve", lambda e, sb_=sb_, d_=d_, ps=ps, g=g: e.scalar_tensor_tensor(
                                out=sb_[:, g * 128:(g + 1) * 128], in0=d_, scalar=aprm[:, 13 + grp * 4 + g:14 + grp * 4 + g],
                                in1=ps[:, g * 128:(g + 1) * 128], op0=ALU.mult, op1=ALU.add),
                                reads=[d_b, pb_, apb], writes=[sb_b])
                        pf, pfb = pfr.next()
                        S.op("act", lambda e, pf=pf, sb_=sb_: e.activation(out=pf, in_=sb_, func=AF.Exp, scale=0.125),
                             reads=[sb_b], writes=[pfb])
                        pm, pmb = pmr.next()
                        mi = 1 if kbI == Bq else 0
                        S.op("dve", lambda e, pm=pm, pf=pf, mi=mi: e.tensor_tensor(out=pm, in0=pf, in1=cst[:, mi, :], op=ALU.mult),
                             reads=[pfb, cstb], writes=[pmb])
                        pms.append((pm, pmb, kbI))
                    return pms

                def back(Bq, pms):
                    for i, (pm, pmb, kbI) in enumerate(pms):
                        S.op("pe", lambda e, pm=pm, kbI=kbI, i=i, n=len(pms): e.matmul(
                            pso[0][:64, :], VAp[:, kbI, :], pm, start=(i == 0), stop=(i == n - 1)),
                            reads=[vapb[kbI // 4], pmb], writes=[pso[1]])
                    for i, (pm, pmb, kbI) in enumerate(pms):
                        S.op("pe", lambda e, pm=pm, i=i, n=len(pms): e.matmul(
                            pssum[0][:64, :], onesb[:, :64], pm, start=(i == 0), stop=(i == n - 1)),
                            reads=[cb_, pmb], writes=[pssum[1]])
                    S.op("dve", lambda e: e.tensor_tensor(out=den[:64, :], in0=pssum[0][:64, :], in1=esink[:64, :], op=ALU.add),
                         reads=[pssum[1], esb], writes=[denb])
                    S.op("dve", lambda e: e.reciprocal(out=den[:64, :], in_=den[:64, :]), reads=[denb], writes=[denb])
                    o_, o_b = osr.next()
                    S.op("dve", lambda e, o_=o_: e.tensor_tensor(out=o_[:64, :], in0=pso[0][:64, :], in1=den[:64, :], op=ALU.mult),
                         reads=[pso[1], denb], writes=[o_b])
                    S.dma("sp", Os[grp * 256:(grp + 1) * 256, Bq * 128:(Bq + 1) * 128].rearrange("(g d) q -> d g q", d=64),
                          o_[:64, :].rearrange("p (g q) -> p g q", q=128), reads=[o_b], writes=[osb], cb=o_b)

                prev = None
                for Bq in range(NB):
                    if Bq % 4 == 0:
                        prep(Bq // 4)
                    cur = front(Bq)
                    if prev is not None:
                        back(Bq - 1, prev)
                    prev = cur
                back(NB - 1, prev)

            def mla(hd):
                A.off = mark
                Kn = A.alloc(T, BF16); Kr = A.alloc(T, BF16)
                Vp = A.alloc(NB * 128, BF16).rearrange("p (b d) -> p b d", d=128)
                knb = [Buf("Kn%d" % t) for t in range(NTILE)]; krb = [Buf("Kr%d" % t) for t in range(NTILE)]
                vpb = [Buf("Vp%d" % t) for t in range(NTILE)]
                cq32 = A.alloc(4 * 512, F32).rearrange("p (c t) -> p c t", t=512); cqb = Buf("cq32")
                ckv32 = A.alloc(2 * 512, F32).rearrange("p (c t) -> p c t", t=512); ckvb = Buf("ckv32")
                kr32 = A.alloc(512, F32); kr32b = Buf("kr32")
                cqn = A.alloc(4 * 512, BF16).rearrange("p (c t) -> p c t", t=512); cqnb = Buf("cqn")
                ckvn = A.alloc(2 * 512, BF16).rearrange("p (c t) -> p c t", t=512); ckvnb = Buf("ckvn")
                cos_ = A.alloc(512, F32); sin_ = A.alloc(512, F32); csb = Buf("cossin")
                ang = A.alloc(512, F32); angb = Buf("ang")
                ki32 = A.alloc(512, I32); kib = Buf("ki32")
                Qn = [A.alloc(512, BF16) for _ in range(2)]; Qr = [A.alloc(512, BF16) for _ in range(2)]
                qnb = [Buf("Qn%d" % i) for i in range(2)]; qrb = [Buf("Qr%d" % i) for i in range(2)]
                tring = mk_tiles(4, "mt")
                rs = A.alloc(512, F32); rsb = Buf("mrs")
                r32 = A.alloc(512, F32); r32b = Buf("r32")
                t1 = A.alloc(512, F32); t1b = Buf("t1"); t2 = A.alloc(512, F32); t2b = Buf("t2")
                pft = [A.alloc(512, F32) for _ in range(2)]
                pfr = Ring([(pft[i], Buf("mpf%d" % i)) for i in range(2)])
                pmt = [A.alloc(512, BF16) for _ in range(4)]
                pmr = Ring([(pmt[i], Buf("mpm%d" % i)) for i in range(4)])
                ost = [A.alloc(512, F32) for _ in range(2)]
                osr = Ring([(ost[i], Buf("mos%d" % i)) for i in range(2)])
                rec = A.alloc(512, F32); recb = Buf("rec")
                psS = Ring([(psum[i], psb[i]) for i in range(3)])
                psP = Ring([(psum[3], psb[3]), (psum[4], psb[4])])
                pso = (psum[5], psb[5]); pssum = (psum[6], psb[6]); pst = (psum[7], psb[7])
                scale = 192.0 ** -0.5
                C1 = 6.28125
                C2 = 2 * PI - 6.28125
                BND = 3.141592

                def rope(src, srcb, dst_ap, dstb):
                    ps, pb_ = psP.next()
                    S.op("pe", lambda e, ps=ps: e.matmul(ps[:64, :], rotT[:64, :64], src[:64, :], start=True, stop=True),
                         reads=[srcb, cstb], writes=[pb_])
                    S.op("dve", lambda e: e.tensor_tensor(out=t1[:64, :], in0=src[:64, :], in1=cos_[:64, :], op=ALU.mult),
                         reads=[srcb, csb], writes=[t1b])
                    S.op("dve", lambda e, ps=ps: e.tensor_tensor(out=t2[:64, :], in0=ps[:64, :], in1=sin_[:64, :], op=ALU.mult),
                         reads=[pb_, csb], writes=[t2b])
                    S.op("dve", lambda e: e.tensor_tensor(out=dst_ap, in0=t1[:64, :], in1=t2[:64, :], op=ALU.add),
                         reads=[t1b, t2b], writes=[dstb])

                def prep(ti):
                    t0 = ti * TT
                    par = ti % 2
                    S.dma("sp", cq32, Zs[1536:2048, t0:t0 + TT].rearrange("(c p) t -> p c t", p=128), reads=[zb], writes=[cqb])
                    S.dma("sp", ckv32, Zs[2048:2304, t0:t0 + TT].rearrange("(c p) t -> p c t", p=128), reads=[zb], writes=[ckvb])
                    S.dma("sp", kr32[:64, :], Zs[2304:2368, t0:t0 + TT], reads=[zb], writes=[kr32b])
                    load_pos(t0)
                    yield
                    S.op("dve", lambda e: e.tensor_scalar(out=ang[:64, :], in0=posF[:64, :], scalar1=aprm[:64, 12:13],
                                                          scalar2=None, op0=ALU.mult), reads=[posb, apb], writes=[angb])
                    S.op("dve", lambda e: e.tensor_scalar(out=t1[:64, :], in0=ang[:64, :], scalar1=1.0 / (2 * PI), scalar2=None,
                                                          op0=ALU.mult), reads=[angb], writes=[t1b])
                    S.op("dve", lambda e: e.tensor_copy(out=ki32[:64, :], in_=t1[:64, :]), reads=[t1b], writes=[kib])
                    S.op("dve", lambda e: e.tensor_copy(out=t1[:64, :], in_=ki32[:64, :]), reads=[kib], writes=[t1b])
                    yield
                    S.op("dve", lambda e: e.scalar_tensor_tensor(out=t2[:64, :], in0=t1[:64, :], scalar=-C1, in1=ang[:64, :],
                                                                 op0=ALU.mult, op1=ALU.add), reads=[t1b, angb], writes=[t2b])
                    S.op("dve", lambda e: e.scalar_tensor_tensor(out=ang[:64, :], in0=t1[:64, :], scalar=-C2, in1=t2[:64, :],
                                                                 op0=ALU.mult, op1=ALU.add), reads=[t1b, t2b], writes=[angb])
                    yield
                    for which, dst in ((0, sin_), (1, cos_)):
                        if which == 1:
                            S.op("dve", lambda e: e.tensor_scalar(out=ang[:64, :], in0=ang[:64, :], scalar1=PI / 2, scalar2=None,
                                                                  op0=ALU.add), reads=[angb], writes=[angb])
                        S.op("dve", lambda e: e.tensor_scalar(out=t1[:64, :], in0=ang[:64, :], scalar1=PI, scalar2=-2 * PI,
                                                              op0=ALU.is_gt, op1=ALU.mult), reads=[angb], writes=[t1b])
                        S.op("dve", lambda e: e.tensor_tensor(out=t1[:64, :], in0=t1[:64, :], in1=ang[:64, :], op=ALU.add),
                             reads=[t1b, angb], writes=[t1b])
                        S.op("dve", lambda e: e.tensor_scalar(out=t1[:64, :], in0=t1[:64, :], scalar1=-BND, scalar2=BND,
                                                              op0=ALU.max, op1=ALU.min), reads=[t1b], writes=[t1b])
                        S.op("act", lambda e, dst=dst: e.activation(out=dst[:64, :], in_=t1[:64, :], func=AF.Sin),
                             reads=[t1b], writes=[csb])
                        yield
                    stat_rstd([(cq32[:, c, :], 128, [cqb]) for c in range(4)], 512, rs, rsb, tring, pst)
                    yield
                    for c in range(4):
                        S.op("dve", lambda e, c=c: e.scalar_tensor_tensor(
                            out=cqn[:, c, :], in0=cq32[:, c, :], scalar=aprm[:, 2 + c:3 + c], in1=rs, op0=ALU.mult, op1=ALU.mult),
                            reads=[cqb, rsb, apb], writes=[cqnb])
                    yield
                    stat_rstd([(ckv32[:, c, :], 128, [ckvb]) for c in range(2)], 256, rs, rsb, tring, pst)
                    yield
                    for c in range(2):
                        S.op("dve", lambda e, c=c: e.scalar_tensor_tensor(
                            out=ckvn[:, c, :], in0=ckv32[:, c, :], scalar=aprm[:, 6 + c:7 + c], in1=rs, op0=ALU.mult, op1=ALU.mult),
                            reads=[ckvb, rsb, apb], writes=[ckvnb])
                    yield
                    pq, pqb = psP.next()
                    for c in range(4):
                        S.op("pe", lambda e, pq=pq, c=c: e.matmul(pq[:, :], wuq[:, c, hd * 192:hd * 192 + 128], cqn[:, c, :],
                                                                  start=(c == 0), stop=(c == 3)), reads=[wb, cqnb], writes=[pqb])
                    pr, prb = psP.next()
                    for c in range(4):
                        S.op("pe", lambda e, pr=pr, c=c: e.matmul(pr[:64, :], wuq[:, c, hd * 192 + 128:hd * 192 + 192], cqn[:, c, :],
                                                                  start=(c == 0), stop=(c == 3)), reads=[wb, cqnb], writes=[prb])
                    yield
                    stat_rstd([(pq[:, :], 128, [pqb]), (pr[:64, :], 64, [prb])], 192, rs, rsb, tring, pst)
                    yield
                    S.op("dve", lambda e, pq=pq: e.scalar_tensor_tensor(
                        out=Qn[par], in0=pq[:, :], scalar=aprm[:, 8:9], in1=rs, op0=ALU.mult, op1=ALU.mult),
                        reads=[pqb, rsb, apb], writes=[qnb[par]])
                    S.op("dve", lambda e, pr=pr: e.scalar_tensor_tensor(
                        out=r32[:64, :], in0=pr[:64, :], scalar=aprm[:64, 9:10], in1=rs[:64, :], op0=ALU.mult, op1=ALU.mult),
                        reads=[prb, rsb, apb], writes=[r32b])
                    yield
                    rope(r32, r32b, Qr[par][:64, :], qrb[par])
                    yield
                    pk, pkb = psP.next()
                    for c in range(2):
                        S.op("pe", lambda e, pk=pk, c=c: e.matmul(pk[:, :], wukv[:, c, hd * 256:hd * 256 + 128], ckvn[:, c, :],
                                                                  start=(c == 0), stop=(c == 1)), reads=[wb, ckvnb], writes=[pkb])
                    yield
                    stat_rstd([(pk[:, :], 128, [pkb]), (kr32[:64, :], 64, [kr32b])], 192, rs, rsb, tring, pst)
                    yield
                    S.op("dve", lambda e, pk=pk: e.scalar_tensor_tensor(
                        out=Kn[:, t0:t0 + TT], in0=pk[:, :], scalar=aprm[:, 10:11], in1=rs, op0=ALU.mult, op1=ALU.mult),
                        reads=[pkb, rsb, apb], writes=[knb[ti]])
                    S.op("dve", lambda e: e.scalar_tensor_tensor(
                        out=r32[:64, :], in0=kr32[:64, :], scalar=aprm[:64, 11:12], in1=rs[:64, :], op0=ALU.mult, op1=ALU.mult),
                        reads=[kr32b, rsb, apb], writes=[r32b])
                    yield
                    rope(r32, r32b, Kr[:64, t0:t0 + TT], krb[ti])
                    yield
                    for tb_ in range(4):
                        pv, pvb = psP.next()
                        for c in range(2):
                            S.op("pe", lambda e, pv=pv, c=c, tb_=tb_: e.matmul(
                                pv[:, :128], ckvn[:, c, tb_ * 128:(tb_ + 1) * 128], wukv[:, c, hd * 256 + 128:hd * 256 + 256],
                                start=(c == 0), stop=(c == 1)), reads=[wb, ckvnb], writes=[pvb])
                        S.op("act", lambda e, pv=pv, tb_=tb_: e.activation(
                            out=Vp[:, 4 * ti + tb_, :], in_=pv[:, :128], func=AF.Copy), reads=[pvb], writes=[vpb[ti]])
                        yield

                def attn(ti, nxt):
                    t0 = ti * TT
                    par = ti % 2
                    nkb = 4 * (ti + 1)
                    LOOK = 2
                    q = []
                    for it in range(nkb + LOOK):
                        if it < nkb:
                            kb = it
                            ps, pb_ = psS.next()
                            S.op("pe", lambda e, ps=ps, kb=kb: e.matmul(ps[:, :], Kn[:, kb * 128:(kb + 1) * 128], Qn[par],
                                                                        start=True, stop=False), reads=[knb[kb // 4], qnb[par]], writes=[pb_])
                            S.op("pe", lambda e, ps=ps, kb=kb: e.matmul(ps[:, :], Kr[:64, kb * 128:(kb + 1) * 128], Qr[par][:64, :],
                                                                        start=False, stop=True), reads=[krb[kb // 4], qrb[par]], writes=[pb_])
                            pm, pmb = pmr.next()
                            if kb >= 4 * ti:
                                pf, pfb = pfr.next()
                                S.op("act", lambda e, pf=pf, ps=ps: e.activation(out=pf, in_=ps[:, :], func=AF.Exp, scale=scale),
                                     reads=[pb_], writes=[pfb])
                                S.op("dve", lambda e, pm=pm, pf=pf, kb=kb: e.tensor_tensor(
                                    out=pm, in0=pf, in1=cst[:, 2 + kb - 4 * ti, :], op=ALU.mult), reads=[pfb, cstb], writes=[pmb])
                            else:
                                S.op("act", lambda e, pm=pm, ps=ps: e.activation(out=pm, in_=ps[:, :], func=AF.Exp, scale=scale),
                                     reads=[pb_], writes=[pmb])
                            q.append((kb, pm, pmb))
                        if it >= LOOK:
                            kb, pm, pmb = q.pop(0)
                            S.op("pe", lambda e, pm=pm, kb=kb: e.matmul(pso[0][:, :], Vp[:, kb, :], pm,
                                                                        start=(kb == 0), stop=(kb == nkb - 1)),
                                 reads=[vpb[kb // 4], pmb], writes=[pso[1]])
                            S.op("pe", lambda e, pm=pm, kb=kb: e.matmul(pssum[0][:, :], onesb, pm,
                                                                        start=(kb == 0), stop=(kb == nkb - 1)),
                                 reads=[cb_, pmb], writes=[pssum[1]])
                            if nxt is not None:
                                next(nxt, None)
                    S.op("dve", lambda e: e.reciprocal(out=rec, in_=pssum[0][:, :]), reads=[pssum[1]], writes=[recb])
                    o_, o_b = osr.next()
                    S.op("dve", lambda e, o_=o_: e.tensor_tensor(out=o_, in0=pso[0][:, :], in1=rec, op=ALU.mult),
                         reads=[pso[1], recb], writes=[o_b])
                    S.dma("sp", Os[1024 + hd * 128:1024 + (hd + 1) * 128, t0:t0 + TT], o_, reads=[o_b], writes=[osb], cb=o_b)

                for _ in prep(0):
                    pass
                for ti in range(NTILE):
                    nxt = prep(ti + 1) if ti + 1 < NTILE else None
                    attn(ti, nxt)
                    if nxt is not None:
                        for _ in nxt:
                            pass

            for grp in range(4):
                swa(grp)
                S.barrier()
            if l + 1 < depth:
                precast(l + 1)
            for h0 in range(8):
                mla(h0)
                S.barrier()

        precast(0)
        for l in range(depth):
            token_phase("first" if l == 0 else "mid", l)
            S.barrier()
            attn_phase(l)
        token_phase("last", depth)
        S.emit(final_bufs=[outb])
    return nc


def _lay(g):
    return np.ascontiguousarray(np.asarray(g, np.float32).reshape(-1, 128).T)


def _host_constants():
    k = np.arange(128)[:, None]; q = np.arange(128)[None, :]
    c = np.zeros((8, 128, 512), np.float32)
    c[0] = np.tile((k > q).astype(np.float32), (1, 4))
    c[1] = np.tile((k <= q).astype(np.float32), (1, 4))
    qq = np.arange(512)[None, :]
    for kb in range(4):
        c[2 + kb] = ((kb * 128 + k) <= qq).astype(np.float32)
    R = np.zeros((64, 64), np.float32)
    for m in range(32):
        R[m + 32, m] = -1.0
        R[m, m + 32] = 1.0
    c[6, :64, :64] = R
    return c


def kernel(x, p, positions, ffn1_norm, ffn1_w_gate, ffn1_w_up, ffn1_w_down, mix_norm, w_in, swa_q_norm,
           swa_k_norm, swa_sinks, mla_q_lora_norm, mla_w_uq, mla_kv_lora_norm, mla_w_ukv, mla_q_norm,
           mla_k_norm, out_norm_swa, out_norm_mla, w_o, ffn2_norm, ffn2_w_gate, ffn2_w_up, ffn2_w_down,
           ple_proj, ple_proj_norm, ple_gate_norm, ple_gate):
    x = np.asarray(x); p = np.asarray(p); positions = np.asarray(positions)
    B, T, _ = x.shape
    depth = p.shape[0]
    f = lambda a: np.ascontiguousarray(np.asarray(a, np.float32))
    gains = np.stack([np.concatenate([
        _lay(ffn1_norm[l]), _lay(mix_norm[l]),
        _lay(np.concatenate([np.asarray(out_norm_swa[l]), np.asarray(out_norm_mla[l])])),
        _lay(ffn2_norm[l]), _lay(ple_gate_norm[l]), _lay(ple_proj_norm[l])], axis=1) for l in range(depth)])
    half = 32
    inv_freq = (10000.0 ** (-np.arange(half, dtype=np.float32) / half)).astype(np.float32)
    slopes = (2.0 ** (-8.0 * np.arange(1, 17, dtype=np.float32) / 16)).astype(np.float32)
    aprm = np.zeros((depth, 128, 32), np.float32)
    for l in range(depth):
        a = aprm[l]
        a[:64, 0] = np.asarray(swa_q_norm[l]); a[:64, 1] = np.asarray(swa_k_norm[l])
        a[:, 2:6] = _lay(mla_q_lora_norm[l]); a[:, 6:8] = _lay(mla_kv_lora_norm[l])
        qn = np.asarray(mla_q_norm[l]); kn = np.asarray(mla_k_norm[l])
        a[:, 8] = qn[:128]; a[:64, 9] = qn[128:]; a[:, 10] = kn[:128]; a[:64, 11] = kn[128:]
        a[:32, 12] = inv_freq; a[32:64, 12] = inv_freq
        a[:, 13:29] = (-8.0 * slopes)[None, :]
        a[:, 30] = np.float32(PI)
    consts = _host_constants()
    sinkrow = np.zeros((depth, 4, 64, 512), np.float32)
    for l in range(depth):
        s = np.asarray(swa_sinks[l], np.float32)
        for grp in range(4):
            sinkrow[l, grp] = np.repeat(s[grp * 4:(grp + 1) * 4], 128)[None, :]
    f32 = lambda a: np.ascontiguousarray(np.asarray(a, np.float32))
    wnames = dict(f1g=ffn1_w_gate, f1u=ffn1_w_up, f1d=ffn1_w_down, w_in=w_in, wuq=mla_w_uq, wukv=mla_w_ukv,
                  w_o=w_o, f2g=ffn2_w_gate, f2u=ffn2_w_up, f2d=ffn2_w_down, pproj=ple_proj, pgate=ple_gate)
    wnames = {k: np.asarray(v) for k, v in wnames.items()}
    nc = build_program(T, 1)
    cur = [np.ascontiguousarray(x[b].T) for b in range(B)]
    posrow = [np.ascontiguousarray(np.broadcast_to(positions[b][None, :], (128, T))).astype(np.int32) for b in range(B)]
    poscol = [np.ascontiguousarray(positions[b].reshape(T // 128, 128).T).astype(np.int32) for b in range(B)]
    for l in range(depth):
        shared = {k: f32(v[l:l + 1]) for k, v in wnames.items()}
        shared.update(gains=f32(gains[l:l + 1]), consts=f32(consts), sinkrow=f32(sinkrow[l:l + 1]), aprm=f32(aprm[l:l + 1]))
        in_maps = []
        for b in range(B):
            m = dict(shared)
            m["xT"] = cur[b]
            m["pT"] = np.ascontiguousarray(np.transpose(p[l:l + 1, b], (0, 2, 1)))
            m["posrow"] = posrow[b]
            m["poscol"] = poscol[b]
            in_maps.append(m)
        res = run_bass_kernel_spmd(nc, in_maps, core_ids=list(range(B)))
        cur = [np.ascontiguousarray(res.results[b]["yT"]) for b in range(B)]
    out = np.stack([np.ascontiguousarray(cur[b].T) for b in range(B)])
    return out.astype(np.float32)
```
